# Optimizing a Trainium2 kernel written in Bass

```python
import math
import jax, jax.numpy as jnp
from jax import lax
import numpy as np

D_MODEL = 1024
BATCH = 16
SEQ = 2048
DEPTH = 4

D_MIX = 2 * D_MODEL
HEAD_DIM = 64
A_WIDTH = D_MIX // 4
A_HEADS = A_WIDTH // HEAD_DIM
B_WIDTH = D_MIX // 4
B_HEADS = B_WIDTH // HEAD_DIM
C_WIDTH = D_MIX // 2
C_HEADS = C_WIDTH // HEAD_DIM
CONV_A_WIDTH = 31
GMLP_CHUNK = 128
SSM_STATE = 128
SSM_GROUPS = 2
SSM_CONV = 4
SSD_CHUNK = 128
D_CONV_C = C_WIDTH + 2 * SSM_GROUPS * SSM_STATE
D_IN_PROJ = 2 * A_WIDTH + 2 * B_WIDTH + C_WIDTH + D_CONV_C + C_HEADS
D_FF = 4 * D_MODEL
EPS = 1e-5

kernel_name = "hybrid_conv_gmlp_ssd_trunk"


def rmsnorm(x, g):
    xf = x.astype(jnp.float32)
    y = xf * lax.rsqrt(jnp.mean(xf * xf, axis=-1, keepdims=True) + EPS)
    return (y * g.astype(jnp.float32)).astype(x.dtype)


def head_layernorm(x, g, b, n_heads):
    lead = x.shape[:-1]
    xf = x.astype(jnp.float32).reshape(*lead, n_heads, -1)
    mu = jnp.mean(xf, axis=-1, keepdims=True)
    xc = xf - mu
    var = jnp.mean(xc * xc, axis=-1, keepdims=True)
    y = (xc * lax.rsqrt(var + EPS)).reshape(*lead, -1)
    return (y * g.astype(jnp.float32) + b.astype(jnp.float32)).astype(x.dtype)


def causal_dwconv(x, w, b):
    k, c = w.shape
    y = lax.conv_general_dilated(
        x, w[:, None, :].astype(x.dtype), window_strides=(1,), padding=[(k - 1, 0)],
        dimension_numbers=("NWC", "WIO", "NWC"), feature_group_count=c)
    return y + b.astype(x.dtype)


def conformer_mixer(a_val, a_gate, conv_w, conv_b, ln_g, ln_b):
    h = a_val * jax.nn.sigmoid(a_gate)
    h = causal_dwconv(h, conv_w, conv_b)
    h = head_layernorm(h, ln_g, ln_b, A_HEADS)
    return jax.nn.silu(h)


def gmlp_mixer(u, v, ln_g, ln_b, w_s, b_s):
    u = jax.nn.gelu(u, approximate=False)
    v = jax.nn.gelu(v, approximate=False)
    v = head_layernorm(v, ln_g, ln_b, B_HEADS)
    bsz, s, _ = v.shape
    nc = s // GMLP_CHUNK
    v = v.reshape(bsz, nc, GMLP_CHUNK, B_HEADS, HEAD_DIM)
    mask = jnp.tril(jnp.ones((GMLP_CHUNK, GMLP_CHUNK), dtype=bool))
    w = jnp.where(mask, w_s, jnp.zeros_like(w_s))
    mix = jnp.einsum("hts,bcshp->bcthp", w, v) + b_s.T[:, :, None]
    return (u.reshape(v.shape) * mix).reshape(bsz, s, B_WIDTH)


def ssd_chunked(x, dt, A, B, C):
    bsz, s, h, p = x.shape
    g, n = B.shape[2], B.shape[3]
    hg = h // g
    nc = s // SSD_CHUNK
    l = SSD_CHUNK
    xc = (x * dt[..., None]).reshape(bsz, nc, l, g, hg, p)
    a_cs = jnp.cumsum((dt * A).reshape(bsz, nc, l, g, hg), axis=2)
    Bc = B.reshape(bsz, nc, l, g, n)
    Cc = C.reshape(bsz, nc, l, g, n)
    a_t = jnp.moveaxis(a_cs, 2, -1)
    seg = a_t[..., :, None] - a_t[..., None, :]
    mask = jnp.tril(jnp.ones((l, l), dtype=bool))
    L = jnp.exp(jnp.where(mask, seg, -jnp.inf))
    CB = jnp.einsum("bclgn,bcsgn->bcgls", Cc, Bc)
    y_diag = jnp.einsum("bcghls,bcsghp->bclghp", CB[:, :, :, None] * L, xc)
    decay_states = jnp.exp(a_cs[:, :, -1:] - a_cs)
    states = jnp.einsum("bclgn,bclghp->bcghpn", Bc, xc * decay_states[..., None])
    chunk_decay = jnp.exp(a_cs[:, :, -1])

    def step(carry, inp):
        st, dec = inp
        return carry * dec[..., None, None] + st, carry

    init = jnp.zeros((bsz, g, hg, p, n), x.dtype)
    _, prev = lax.scan(step, init, (jnp.moveaxis(states, 1, 0), jnp.moveaxis(chunk_decay, 1, 0)))
    prev = jnp.moveaxis(prev, 0, 1)
    y_off = jnp.einsum("bclgn,bcghpn->bclghp", Cc, prev) * jnp.exp(a_cs)[..., None]
    return (y_diag + y_off).reshape(bsz, s, h, p)


def mamba2_mixer(z, xbc, dt_raw, conv_w, conv_b, dt_bias, a_log, d_skip, norm_g):
    xbc = jax.nn.silu(causal_dwconv(xbc, conv_w, conv_b))
    xs, Bm, Cm = jnp.split(xbc, [C_WIDTH, C_WIDTH + SSM_GROUPS * SSM_STATE], axis=-1)
    bsz, s, _ = xs.shape
    xs = xs.astype(jnp.float32).reshape(bsz, s, C_HEADS, HEAD_DIM)
    Bm = Bm.astype(jnp.float32).reshape(bsz, s, SSM_GROUPS, SSM_STATE)
    Cm = Cm.astype(jnp.float32).reshape(bsz, s, SSM_GROUPS, SSM_STATE)
    dt = jax.nn.softplus(dt_raw.astype(jnp.float32) + dt_bias.astype(jnp.float32))
    A = -jnp.exp(a_log.astype(jnp.float32))
    y = ssd_chunked(xs, dt, A, Bm, Cm) + d_skip.astype(jnp.float32)[:, None] * xs
    y = y.reshape(bsz, s, C_WIDTH) * jax.nn.silu(z.astype(jnp.float32))
    yg = y.reshape(bsz, s, SSM_GROUPS, -1)
    yg = yg * lax.rsqrt(jnp.mean(yg * yg, axis=-1, keepdims=True) + EPS)
    y = yg.reshape(bsz, s, C_WIDTH) * norm_g.astype(jnp.float32)
    return y.astype(z.dtype)


def setup_inputs(seed: int = 0) -> dict:
    key = jax.random.key(seed)
    ks = jax.random.split(key, 24)
    f32 = jnp.float32

    def nrm(k, shape, scale):
        return jax.random.normal(k, shape, f32) * scale

    dt0 = jnp.exp(jax.random.uniform(ks[13], (DEPTH, C_HEADS), f32) * (math.log(0.1) - math.log(0.001)) + math.log(0.001))
    return {
        "x": nrm(ks[0], (BATCH, SEQ, D_MODEL), 1.0),
        "norm1_g": 1.0 + nrm(ks[1], (DEPTH, D_MODEL), 0.02),
        "w_in": nrm(ks[2], (DEPTH, D_MODEL, D_IN_PROJ), D_MODEL ** -0.5),
        "conv_a_w": nrm(ks[3], (DEPTH, CONV_A_WIDTH, A_WIDTH), CONV_A_WIDTH ** -0.5),
        "conv_a_b": nrm(ks[4], (DEPTH, A_WIDTH), 0.02),
        "ln_a_g": 1.0 + nrm(ks[5], (DEPTH, A_WIDTH), 0.02),
        "ln_a_b": nrm(ks[6], (DEPTH, A_WIDTH), 0.02),
        "ln_b_g": 1.0 + nrm(ks[7], (DEPTH, B_WIDTH), 0.02),
        "ln_b_b": nrm(ks[8], (DEPTH, B_WIDTH), 0.02),
        "w_spatial": nrm(ks[9], (DEPTH, B_HEADS, GMLP_CHUNK, GMLP_CHUNK), GMLP_CHUNK ** -0.5),
        "b_spatial": 1.0 + nrm(ks[10], (DEPTH, B_HEADS, GMLP_CHUNK), 0.1),
        "conv_c_w": nrm(ks[11], (DEPTH, SSM_CONV, D_CONV_C), SSM_CONV ** -0.5),
        "conv_c_b": nrm(ks[12], (DEPTH, D_CONV_C), 0.02),
        "dt_bias": dt0 + jnp.log(-jnp.expm1(-dt0)),
        "a_log": jnp.log(jax.random.uniform(ks[14], (DEPTH, C_HEADS), f32, 1.0, 16.0)),
        "d_skip": 1.0 + nrm(ks[15], (DEPTH, C_HEADS), 0.1),
        "norm_c_g": 1.0 + nrm(ks[16], (DEPTH, C_WIDTH), 0.02),
        "w_out": nrm(ks[17], (DEPTH, D_MIX, D_MODEL), D_MIX ** -0.5),
        "norm2_g": 1.0 + nrm(ks[18], (DEPTH, D_MODEL), 0.02),
        "w_ff1": nrm(ks[19], (DEPTH, D_MODEL, D_FF), D_MODEL ** -0.5),
        "w_ff2": nrm(ks[20], (DEPTH, D_FF, D_MODEL), D_FF ** -0.5),
        "final_g": 1.0 + nrm(ks[21], (D_MODEL,), 0.02),
    }


def reference(x, norm1_g, w_in, conv_a_w, conv_a_b, ln_a_g, ln_a_b, ln_b_g, ln_b_b,
              w_spatial, b_spatial, conv_c_w, conv_c_b, dt_bias, a_log, d_skip, norm_c_g,
              w_out, norm2_g, w_ff1, w_ff2, final_g):
    split_idx = [A_WIDTH, 2 * A_WIDTH, 2 * A_WIDTH + B_WIDTH, 2 * A_WIDTH + 2 * B_WIDTH,
                 2 * A_WIDTH + 2 * B_WIDTH + C_WIDTH, 2 * A_WIDTH + 2 * B_WIDTH + C_WIDTH + D_CONV_C]
    for i in range(DEPTH):
        h = rmsnorm(x, norm1_g[i])
        proj = h @ w_in[i]
        a_val, a_gate, b_u, b_v, z, xbc, dt_raw = jnp.split(proj, split_idx, axis=-1)
        ya = conformer_mixer(a_val, a_gate, conv_a_w[i], conv_a_b[i], ln_a_g[i], ln_a_b[i])
        yb = gmlp_mixer(b_u, b_v, ln_b_g[i], ln_b_b[i], w_spatial[i], b_spatial[i])
        yc = mamba2_mixer(z, xbc, dt_raw, conv_c_w[i], conv_c_b[i], dt_bias[i], a_log[i],
                          d_skip[i], norm_c_g[i])
        x = x + jnp.concatenate([ya, yb, yc], axis=-1) @ w_out[i]
        h = rmsnorm(x, norm2_g[i])
        x = x + jnp.square(jax.nn.relu(h @ w_ff1[i])) @ w_ff2[i]
    return rmsnorm(x, final_g)
```

```python
import numpy as np
from contextlib import ExitStack
import concourse.bass as bass
import concourse.mybir as mybir
from concourse.bass_utils import run_bass_kernel_spmd

F32 = mybir.dt.float32
BF16 = mybir.dt.bfloat16
ALU = mybir.AluOpType
AF = mybir.ActivationFunctionType
AX = mybir.AxisListType
DSZ = {F32: 4, BF16: 2}
COMPUTE = ("pe", "act", "dve", "pool")
ALLENG = COMPUTE + ("sp",)
EPS = 1e-5


def _region(ap):
    name = ap.tensor.name
    sz = DSZ.get(ap.dtype, 4)
    dims = ap.ap
    off = ap.offset
    if "dram" in str(ap.space).lower() or "hbm" in str(ap.space).lower():
        ext = 1
        for s, c in dims:
            ext += (c - 1) * abs(s)
        return (name, 0, 1, off * sz, (off + ext) * sz)
    if name.startswith("ps"):
        return (name, 0, 128, 0, 2048)
    pstep = dims[0][0]
    npart = dims[0][1]
    if pstep == 0:
        pstep = 1 << 40
    p0 = off // pstep
    f0 = off % pstep
    ext = 1
    for s, c in dims[1:]:
        ext += (c - 1) * abs(s)
    return (name, p0, p0 + npart, f0 * sz, (f0 + ext) * sz)


def _ovl(a, b):
    return a[1] < b[2] and b[1] < a[2] and a[3] < b[4] and b[3] < a[4]


def _cov(a, b):
    return a[1] <= b[1] and a[2] >= b[2] and a[3] <= b[3] and a[4] >= b[4]


class Op:
    __slots__ = ("eng", "fn", "deps", "idx", "inc", "count", "grp", "waits")


class Prog:
    def __init__(self, nc):
        self.nc = nc
        self.ops = []
        self.W = {}
        self.R = {}
        self.groups = []

    def add(self, eng, fn, reads=(), writes=(), grp=None):
        op = Op()
        op.eng = eng
        op.fn = fn
        op.idx = len(self.ops)
        op.inc = False
        op.count = 0
        op.grp = grp
        op.waits = None
        if grp is not None and grp not in self.groups:
            self.groups.append(grp)
        deps = set()
        ordered = grp is None
        for ap in reads:
            r = _region(ap)
            wl = self.W.get(r[0])
            if wl:
                for e in wl:
                    if _ovl(e[0], r):
                        d = e[1]
                        if ordered and d.grp is None and d.eng == eng and eng == "pe":
                            continue
                        deps.add(d.idx)
            rl = self.R.setdefault(r[0], [])
            if ordered:
                rl[:] = [e for e in rl if not (e[1].grp is None and e[1].eng == eng and _cov(r, e[0]))]
            rl.append((r, op))
        for ap in writes:
            r = _region(ap)
            wl = self.W.setdefault(r[0], [])
            for e in wl:
                if _ovl(e[0], r):
                    d = e[1]
                    if ordered and d.grp is None and d.eng == eng and eng == "pe":
                        continue
                    deps.add(d.idx)
            rl = self.R.setdefault(r[0], [])
            for e in rl:
                if _ovl(e[0], r):
                    d = e[1]
                    if d is op:
                        continue
                    if ordered and d.grp is None and d.eng == eng and eng == "pe":
                        continue
                    deps.add(d.idx)
            wl[:] = [e for e in wl if not _cov(r, e[0])]
            rl[:] = [e for e in rl if not (_cov(r, e[0]) and e[1] is not op)]
            wl.append((r, op))
        best = {}
        keep = set()
        for d in deps:
            dop = self.ops[d]
            if dop.grp is None:
                if d > best.get(dop.eng, -1):
                    best[dop.eng] = d
            else:
                keep.add(d)
        keep.update(best.values())
        op.deps = keep
        self.ops.append(op)
        return op

    def mm(self, out, lhsT, rhs, start=True, stop=True):
        return self.add("pe", lambda e: e.matmul(out, lhsT, rhs, start=start, stop=stop),
                        reads=[lhsT, rhs], writes=[out])

    def transpose(self, out, in_, ident):
        return self.add("pe", lambda e: e.transpose(out, in_, ident), reads=[in_, ident], writes=[out])

    def act(self, out, in_, func, bias=0.0, scale=1.0, accum_out=None):
        reads = [in_]
        if not isinstance(bias, (int, float)):
            reads.append(bias)
        if not isinstance(scale, (int, float)):
            reads.append(scale)
        writes = [out]
        kw = {}
        if accum_out is not None:
            kw["accum_out"] = accum_out
            writes.append(accum_out)
        return self.add("act", lambda e: e.activation(out, in_, func, bias=bias, scale=scale, **kw),
                        reads=reads, writes=writes)

    def tt(self, out, in0, in1, op, eng="dve"):
        return self.add(eng, lambda e: e.tensor_tensor(out, in0, in1, op), reads=[in0, in1], writes=[out])

    def ts(self, out, in0, s1, s2, op0, op1=None, eng="dve"):
        reads = [in0]
        if not isinstance(s1, (int, float)):
            reads.append(s1)
        if s2 is not None and not isinstance(s2, (int, float)):
            reads.append(s2)
        kw = {}
        if op1 is not None:
            kw["op1"] = op1
        return self.add(eng, lambda e: e.tensor_scalar(out, in0, s1, s2, op0, **kw), reads=reads, writes=[out])

    def stt(self, out, in0, scalar, in1, op0, op1, eng="dve"):
        reads = [in0, in1]
        if not isinstance(scalar, (int, float)):
            reads.append(scalar)
        return self.add(eng, lambda e: e.scalar_tensor_tensor(out, in0, scalar, in1, op0, op1),
                        reads=reads, writes=[out])

    def copy(self, out, in_, eng="dve"):
        if eng == "act":
            return self.add(eng, lambda e: e.copy(out, in_), reads=[in_], writes=[out])
        return self.add(eng, lambda e: e.tensor_copy(out, in_), reads=[in_], writes=[out])

    def recip(self, out, in_):
        return self.add("dve", lambda e: e.reciprocal(out, in_), reads=[in_], writes=[out])

    def reduce(self, out, in_, eng="dve"):
        return self.add(eng, lambda e: e.tensor_reduce(out, in_, AX.X, ALU.add), reads=[in_], writes=[out])

    def memset(self, ap, val, eng="pool"):
        return self.add(eng, lambda e: e.memset(ap, val), reads=[], writes=[ap])

    def asel(self, ap, pattern, cmp, fill, base=0, cm=0):
        return self.add("pool", lambda e: e.affine_select(ap, ap, pattern, cmp, fill, base=base,
                                                          channel_multiplier=cm), reads=[ap], writes=[ap])

    def dma(self, out, in_, eng="sp", grp="g0"):
        return self.add(eng, lambda e: e.dma_start(out, in_), reads=[in_], writes=[out], grp=grp)

    def finalize(self):
        ops = self.ops
        for op in ops:
            for d in op.deps:
                if ops[d].grp is None:
                    ops[d].inc = True
        cnt = {e: 0 for e in ALLENG}
        tot = {g: 0 for g in self.groups}
        seen = {e: {} for e in ALLENG}
        know = {e: {} for e in ALLENG}
        nw = 0
        for op in ops:
            w = {}
            for d in op.deps:
                dop = ops[d]
                if dop.grp is None:
                    key = ("e", dop.eng)
                    val = dop.count
                else:
                    key = ("g", dop.grp)
                    val = tot[dop.grp]
                if val > w.get(key, 0):
                    w[key] = val
            s = seen[op.eng]
            op.waits = []
            for key, val in sorted(w.items(), key=lambda kv: -kv[1]):
                if val > s.get(key, 0):
                    s[key] = val
                    op.waits.append((key, val))
                    nw += 1
                    if key[0] == "e":
                        kd = know[key[1]].get(val)
                        if kd:
                            for k2, v2 in kd.items():
                                if v2 > s.get(k2, 0):
                                    s[k2] = v2
            if op.grp is None:
                if op.inc:
                    cnt[op.eng] += 1
                    kd = dict(s)
                    kd[("e", op.eng)] = cnt[op.eng]
                    know[op.eng][cnt[op.eng]] = kd
                op.count = cnt[op.eng]
            else:
                tot[op.grp] += 16
        self.stats = dict(n_ops=len(ops), n_waits=nw, cnt=cnt, ngroups=len(self.groups))

    def emit(self):
        nc = self.nc
        self.finalize()
        with ExitStack() as es:
            sems = {}
            for e in ALLENG:
                sems[("e", e)] = es.enter_context(nc.semaphore("s_" + e))
            for g in self.groups:
                sems[("g", g)] = es.enter_context(nc.semaphore("g_" + str(g)))
            block = es.enter_context(nc.Block())
            by_eng = {e: [] for e in ALLENG}
            for op in self.ops:
                by_eng[op.eng].append(op)

            def run(engobj, name):
                for op in by_eng[name]:
                    for key, val in op.waits:
                        engobj.wait_ge(sems[key], val)
                    ins = op.fn(engobj)
                    if ins is None:
                        continue
                    if op.grp is not None:
                        ins.then_inc(sems[("g", op.grp)], 16)
                    elif op.inc:
                        ins.then_inc(sems[("e", name)], 1)

            @block.sync
            def _(e):
                run(e, "sp")

            @block.tensor
            def _(e):
                run(e, "pe")

            @block.scalar
            def _(e):
                run(e, "act")

            @block.vector
            def _(e):
                run(e, "dve")

            @block.gpsimd
            def _(e):
                run(e, "pool")


def bc(ap, counts):
    dims = [list(d) for d in ap.ap]
    for c in counts:
        dims.append([0, c])
    return bass.AP(ap.tensor, ap.offset, dims)


D = 1024
DIN = 4624
NCOLS = 220
C_G1, C_G2, C_CAB, C_LAG, C_LAB, C_CCB, C_BST, C_NCG, C_CAW, C_CCW = 0, 8, 16, 20, 24, 28, 32, 40, 48, 172
NROWS = 512 + 512 + 1536 + 48
R_LBG, R_LBB, R_CCB, R_DTB, R_ALOG, R_DSK = 0, 512, 1024, 2560, 2576, 2592
T = 512
NCH = 4
NSLOT = 6
LOOKAHEAD = 4


def build_nc(NSEQ, S, DEPTH):
    nc = bass.Bass("TRN2", target_bir_lowering=False)
    P = Prog(nc)
    NT = S // T

    def din(name, shape, dt=F32):
        return nc.dram_tensor(name, shape, dt, kind="ExternalInput").ap()

    x_d = din("x", [NSEQ, D, S])
    w_in_d = din("w_in", [DEPTH, D, DIN])
    w_out_d = din("w_out", [DEPTH, 2048, D])
    w_ff1_d = din("w_ff1", [DEPTH, D, 4096])
    w_ff2_d = din("w_ff2", [DEPTH, 4096, D])
    cols_d = din("cols", [DEPTH, 128, NCOLS])
    rows_d = din("rows", [DEPTH, NROWS])
    wst_d = din("wst", [DEPTH, 128, 1024])
    fin_d = din("fin", [128, 8])
    out_d = nc.dram_tensor("out", [NSEQ, D, S], F32, kind="ExternalOutput").ap()
    diagA_d = nc.dram_tensor("diagA", [DEPTH, 4, 128, 31 * 128], BF16).ap()
    diagC_d = nc.dram_tensor("diagC", [DEPTH, 128, 48 * 128], BF16).ap()
    NPAN = 31
    wbf_d = nc.dram_tensor("wbf", [DEPTH, NPAN, 128, 4096], BF16).ap()
    wdtb_d = nc.dram_tensor("wdtb", [DEPTH, 128, 128], BF16).ap()

    def wsrc(t_ap, base, rowlen, nk, ncol, col0):
        return bass.AP(t_ap.tensor, base + col0, [[rowlen, 128], [128 * rowlen, nk], [1, ncol]])

    def panel_specs(l):
        b_in = l * D * DIN
        sp_ = []
        for ct in range(4):
            sp_.append((("A", ct), [128, 8, 256], wsrc(w_in_d, b_in, DIN, 8, 256, ct * 256)))
        for nm, c0 in (("U", 1024), ("V", 1536), ("X0", 3072), ("X1", 3584), ("X2", 4096), ("Z0", 2048), ("Z1", 2560)):
            sp_.append((nm, [128, 8, 512], wsrc(w_in_d, b_in, DIN, 8, 512, c0)))
        for q in range(4):
            sp_.append((("O", q), [128, 4, 1024], wsrc(w_out_d, l * 2048 * D + q * 4 * 128 * D, D, 4, 1024, 0)))
        for j in range(8):
            sp_.append((("F1", j), [128, 8, 512], wsrc(w_ff1_d, l * D * 4096, 4096, 8, 512, j * 512)))
        for q in range(8):
            sp_.append((("F2", q), [128, 4, 1024], wsrc(w_ff2_d, l * 4096 * D + q * 4 * 128 * D, D, 4, 1024, 0)))
        assert len(sp_) == NPAN
        return sp_

    ppdummy = nc.dram_tensor("ppdummy", [8, 16], F32).ap()
    ppi = [0]

    def prepass(l, batch=None):
        def one(dst, src):
            gi = ppi[0] % 8
            ppi[0] += 1
            P.add("pool", lambda e: e.dma_start(dst, src), reads=[src], writes=[dst, ppdummy[gi:gi + 1, :]],
                  grp="pp%d_%d" % (l, gi))
        jobs = []
        for pi, (nm, shape, src) in enumerate(panel_specs(l)):
            n = shape[1] * shape[2]
            jobs.append((wbf_d[l, pi][:, 0:n].rearrange("p (a b) -> p a b", b=shape[2]), src))
        jobs.append((wdtb_d[l].rearrange("p (a b) -> p a b", b=16), wsrc(w_in_d, l * D * DIN, DIN, 8, 16, 4608)))
        for i, (dst, src) in enumerate(jobs):
            if batch is None or i // 8 == batch:
                one(dst, src)

    def sb(name, shape, dt=F32):
        return nc.alloc_sbuf_tensor(name, shape, dt)

    xT = sb("xT", [128, 8, T])
    hT = sb("hT", [128, 8, T], BF16)
    ybuf = sb("ybuf", [128, 16 * T], BF16)
    yT = ybuf[:].rearrange("p (a b) -> p a b", b=T)
    ring = sb("ring", [128, NSLOT, 4096], BF16)
    wdt = sb("wdt", [128, 8, 16], BF16)
    dC = sb("dC", [128, 48 * 128], BF16)
    Sst = sb("Sst", [128, DEPTH, 2, 512])
    Sbf = sb("Sbf", [128, 2, 512], BF16)
    histA = sb("histA", [128, DEPTH, 4, 30], BF16)
    histC = sb("histC", [128, DEPTH, 12, 4], BF16)
    ident = sb("ident", [128, 128])
    identb = sb("identb", [128, 128], BF16)
    ones = sb("ones", [128, 128])
    Bd64 = sb("Bd64", [128, 128])
    Utri = sb("Utri", [128, 128])
    maskneg = sb("maskneg", [128, 128])
    Esel = sb("Esel", [48, 16, 128], BF16)
    maskb = sb("maskb", [128, 128], BF16)
    Bd64b = sb("Bd64b", [128, 128], BF16)
    Utrib = sb("Utrib", [128, 128], BF16)
    onesbb = sb("onesbb", [128, 128], BF16)
    onesb = sb("onesb", [2, 128], BF16)
    acs2 = sb("acs2", [48, 128], BF16)
    wsT = sb("wsT", [128, 1024], BF16)
    cols = sb("cols_sb", [128, DEPTH, NCOLS])
    fin = sb("fin_sb", [128, 8])
    hb16 = sb("hb16", [128, DEPTH, 48])
    lnb = sb("lnb", [128, 1024])
    ccb2 = sb("ccb2", [2, 1536], BF16)
    ccb_d = nc.dram_tensor("ccbd", [DEPTH, 2, 1536], BF16).ap()
    SCRW = 13056
    scr = sb("scr", [128, SCRW])
    wsm_d = nc.dram_tensor("wsm", [DEPTH, 128, 1024], BF16).ap()
    ps_banks = [nc.alloc_psum_tensor("ps%d" % i, [128, 512], F32) for i in range(8)]
    psi = [0]

    def ps():
        b = ps_banks[psi[0] % 8]
        psi[0] += 1
        return b

    soff = [0]

    def carve(nwords, dt=F32, shape=None):
        a = scr[:, soff[0]:soff[0] + nwords]
        soff[0] += nwords
        assert soff[0] <= SCRW, soff[0]
        if dt == BF16:
            a = a.bitcast(BF16)
        return a

    sqA = carve(512); sqB = carve(512); rs = carve(512)
    st8 = carve(64)
    base = soff[0]
    SA = []
    for _ in range(2):
        SA.append(dict(sg=carve(512), hb=carve(272, BF16), cA=carve(512), cA2=carve(512), cAh=carve(256, BF16), msq=carve(512),
                       var=carve(512), t1=carve(512)))
    soff[0] = base
    xbuf = carve(12 * 258, BF16)
    BCcm = carve(4 * 256, BF16)
    xs = carve(1024)
    xc = carve(512, BF16); xcd = carve(512, BF16)
    Btok = carve(128, BF16)
    sm = carve(256)
    CBT = [carve(128), carve(128)]
    LT = [[carve(128), carve(128)], [carve(128), carve(128)]]
    MT = [[carve(64, BF16), carve(64, BF16)], [carve(64, BF16), carve(64, BF16)]]
    y1 = [carve(512), carve(512)]; xd = [carve(512), carve(512)]; sz = [carve(512), carve(512)]
    yct = [carve(256, BF16), carve(256, BF16)]
    c_end = soff[0]
    soff[0] = c_end - 2 * 3136
    assert soff[0] >= base + 12 * 258 + 4 * 256
    SB = []
    for _ in range(2):
        SB.append(dict(gu=carve(512), gv=carve(512), gv2=carve(512), vn=carve(512), vnb=carve(256, BF16),
                       ybt=carve(256, BF16), tmpB=carve(512), st=carve(64)))
    assert soff[0] == c_end
    uT = scr[:, 0:8192].bitcast(BF16)
    rl = [scr[:, 8192:8704], scr[:, 8704:9216]]
    stage = ybuf[:].bitcast(F32)

    xbv = lambda ct, a, b: xbuf[:, ct * 516 + a: ct * 516 + b]
    BCv = lambda i, a, b: BCcm[:, i * 512 + a: i * 512 + b]

    P.memset(ident[:], 0.0)
    P.asel(ident[:], [[-1, 128]], ALU.not_equal, 1.0, base=0, cm=1)
    P.copy(identb[:], ident[:], eng="dve")
    P.memset(ones[:], 1.0)
    P.memset(Bd64[:], 0.0)
    P.memset(Bd64[0:64, 0:64], 1.0 / 64)
    P.memset(Bd64[64:128, 64:128], 1.0 / 64)
    P.memset(Utri[:], 1.0)
    P.asel(Utri[:], [[1, 128]], ALU.is_ge, 0.0, base=0, cm=-1)
    P.memset(maskneg[:], 0.0)
    P.asel(maskneg[:], [[1, 128]], ALU.is_ge, -30000.0, base=0, cm=-1)
    P.memset(Esel[:], 0.0)
    P.asel(Esel[0:16], [[-1, 16], [0, 128]], ALU.not_equal, 1.0, base=0, cm=1)
    P.dma(Esel[32:48], Esel[0:16], eng="sp", grp="c3")
    P.copy(maskb[:], maskneg[:], eng="dve")
    P.copy(Bd64b[:], Bd64[:], eng="dve")
    P.copy(Utrib[:], Utri[:], eng="dve")
    P.memset(onesbb[:], 1.0)
    P.memset(onesb[:], 1.0)
    P.memset(acs2[:], 0.0)
    P.dma(fin[:], fin_d, eng="sp", grp="c0")
    for l in range(DEPTH):
        P.dma(wsT[:], wst_d[l], eng="pool", grp="c1")
        P.asel(wsT[:].rearrange("p (h t) -> p h t", t=128), [[0, 8], [1, 128]], ALU.is_ge, 0.0, base=0, cm=-1)
        P.dma(wsm_d[l], wsT[:], eng="sp", grp="c2")
    prepass(0)
    dtmp = [scr[:, 0:1984].bitcast(BF16), scr[:, 1984:3968].bitcast(BF16)]
    dtmpC = scr[:, 3968:3968 + 3072].bitcast(BF16)
    for l in range(DEPTH):
        P.dma(cols[:, l, :], cols_d[l], eng="sp", grp="c0")
        P.dma(hb16[:, l, :], bass.AP(rows_d.tensor, l * NROWS + R_DTB, [[0, 128], [1, 48]]), eng="sp", grp="c0")
        btmp = scr[0:1, 8000:8000 + 1536]
        bhi = scr[0:1, 9600:9600 + 768].bitcast(BF16)
        blo = scr[0:1, 10400:10400 + 768].bitcast(BF16)
        P.dma(btmp, rows_d[l:l + 1, R_CCB:R_CCB + 1536], eng="sp", grp="c0")
        P.copy(bhi, btmp, eng="act")
        P.tt(blo, btmp, bhi, ALU.subtract)
        P.dma(ccb_d[l, 0:1, :], bhi, eng="sp", grp="c2")
        P.dma(ccb_d[l, 1:2, :], blo, eng="sp", grp="c2")
        P.act(hb16[:, l, 16:32], hb16[:, l, 16:32], AF.Exp)
        P.ts(hb16[:, l, 16:32], hb16[:, l, 16:32], -1.0, None, ALU.mult)
        for ct in range(4):
            dt_ = dtmp[ct % 2]
            for k in range(31):
                c = C_CAW + ct * 31 + k
                if k % 2 == 0:
                    P.act(dt_[:, k * 128:(k + 1) * 128], identb[:], AF.Copy, scale=cols[:, l, c:c + 1])
                else:
                    P.ts(dt_[:, k * 128:(k + 1) * 128], identb[:], cols[:, l, c:c + 1], None, ALU.mult)
            P.dma(diagA_d[l, ct], dt_[:, 0:3968], eng="sp", grp="c2")
        for ct in range(12):
            for k in range(4):
                c = C_CCW + ct * 4 + k
                i = ct * 4 + k
                if i % 2 == 0:
                    P.act(dtmpC[:, i * 128:(i + 1) * 128], identb[:], AF.Copy, scale=cols[:, l, c:c + 1])
                else:
                    P.ts(dtmpC[:, i * 128:(i + 1) * 128], identb[:], cols[:, l, c:c + 1], None, ALU.mult)
        P.dma(diagC_d[l], dtmpC[:, 0:6144], eng="sp", grp="c2")

    reqs = []
    issued = [0]

    def ring_view(i, shape):
        slot = i % NSLOT
        a = ring[:, slot, :]
        if len(shape) == 2:
            return a[:, 0:shape[1]]
        return a[:, 0:shape[1] * shape[2]].rearrange("p (a b) -> p a b", b=shape[2])

    def touch(i, keep=2):
        while issued[0] <= min(i + NSLOT - keep, len(reqs) - 1):
            j = issued[0]
            n, src, eng = reqs[j]
            P.dma(ring[:, j % NSLOT, 0:n], src, eng=eng, grp=("w%d%s" % (j % NSLOT, eng)))
            issued[0] += 1

    def plan_layer(l):
        idx = {}
        specs = {nm: (pi, shape) for pi, (nm, shape, src) in enumerate(panel_specs(l))}

        def req(nm):
            pi, shape = specs[nm]
            n = shape[1] * shape[2]
            idx[nm] = len(reqs)
            reqs.append((n, wbf_d[l, pi][:, 0:n], "sp"))

        def reqd(ct):
            idx[("dA", ct)] = len(reqs)
            reqs.append((3968, diagA_d[l, ct], "sp"))

        for pair in range(2):
            req(("A", 2 * pair)); req(("A", 2 * pair + 1)); reqd(2 * pair); reqd(2 * pair + 1)
        for nm in ("U", "V", "X0", "X1", "X2", "Z0", "Z1"):
            req(nm)
        for q in range(4):
            req(("O", q))
        for j in range(8):
            req(("F1", j))
        for q in range(8):
            req(("F2", q))
        return idx

    def interleave(gens):
        gens = list(gens)
        while gens:
            nxt = []
            for g_ in gens:
                try:
                    next(g_)
                    nxt.append(g_)
                except StopIteration:
                    pass
            gens = nxt

    def rmsnorm_to_hT(l, gcol, gsrc=None):
        pss = ps()
        for kt in range(8):
            sq = (sqA if kt % 2 == 0 else sqB).bitcast(BF16)[:, 0:512]
            if kt % 2 == 0:
                P.act(sq, xT[:, kt, :], AF.Square)
            else:
                P.tt(sq, xT[:, kt, :], xT[:, kt, :], ALU.mult, eng="pool")
            P.mm(pss[:], onesbb[:], sq, start=(kt == 0), stop=(kt == 7))
        P.act(rs, pss[:], AF.Ln, bias=EPS, scale=1.0 / D)
        P.act(rs, rs, AF.Exp, scale=-0.5)
        return rs

    def block(l, first, pidx, hooks=None):
        hooks = hooks or {}
        cl = lambda c0, n=1: cols[:, l, c0:c0 + n]
        P.dma(lnb[:], bass.AP(rows_d.tensor, l * NROWS + R_LBG, [[0, 128], [1, 1024]]), eng="sp", grp="r0")
        P.dma(ccb2[:], ccb_d[l], eng="sp", grp="r1")
        P.dma(dC[:], diagC_d[l], eng="sp", grp="r2")
        P.dma(wsT[:], wsm_d[l], eng="sp", grp="r4")
        P.dma(wdt[:], wdtb_d[l].rearrange("p (a b) -> p a b", b=16), eng="sp", grp="r3")
        rs_ = rmsnorm_to_hT(l, C_G1)
        for kt in range(8):
            P.stt(hT[:, kt, :], xT[:, kt, :], cl(C_G1 + kt), rs_, ALU.mult, ALU.mult)
        if "a" in hooks:
            hooks["a"]()
        def genA(ct):
            S_ = SA[ct % 2]
            sg, hb, cA, cA2, msq, var, t1 = (S_[k] for k in ("sg", "hb", "cA", "cA2", "msq", "var", "t1"))
            ia = pidx[("A", ct)]
            touch(ia, keep=3)
            WA = ring_view(ia, [128, 8, 256])
            pv = ps(); pg = ps()
            for kt in range(8):
                P.mm(pv[:], WA[:, kt, 0:128], hT[:, kt, :], start=(kt == 0), stop=(kt == 7))
            for kt in range(8):
                P.mm(pg[:], WA[:, kt, 128:256], hT[:, kt, :], start=(kt == 0), stop=(kt == 7))
            yield
            P.act(sg, pg[:], AF.Sigmoid)
            if first:
                P.memset(hb[:, 0:30], 0.0)
            else:
                P.copy(hb[:, 0:30], histA[:, l, ct, :], eng="pool")
            P.tt(hb[:, 30:542], pv[:], sg, ALU.mult)
            P.copy(histA[:, l, ct, :], hb[:, 512:542], eng="pool")
            ida = pidx[("dA", ct)]
            touch(ida, keep=3)
            dA = ring_view(ida, [128, 3968])
            pc = ps()
            for k in range(31):
                P.mm(pc[:], dA[:, k * 128:(k + 1) * 128], hb[:, k:k + 512], start=(k == 0), stop=(k == 30))
            yield
            cAh = S_["cAh"]
            cA2b = cA2.bitcast(BF16)[:, 0:512]
            P.act(cA, pc[:], AF.Identity, bias=cl(C_CAB + ct))
            P.act(cA2b, pc[:], AF.Square, bias=cl(C_CAB + ct))
            P.copy(cAh, cA, eng="pool")
            pm = ps(); pe = ps()
            P.mm(pm[:], Bd64b[:], cAh)
            P.mm(pe[:], Bd64b[:], cA2b)
            yield
            P.act(msq, pm[:], AF.Square)
            P.tt(var, pe[:], msq, ALU.subtract)
            P.tt(t1, cA, pm[:], ALU.subtract)
            yield
            P.act(var, var, AF.Ln, bias=EPS)
            P.act(var, var, AF.Exp, scale=-0.5)
            yield
            P.tt(t1, t1, var, ALU.mult, eng="pool")
            yield
            P.act(yT[:, ct, :], t1, AF.Silu, bias=cl(C_LAB + ct), scale=cl(C_LAG + ct))

        for pair in range(2):
            interleave([genA(2 * pair), genA(2 * pair + 1)])

        if "b" in hooks:
            hooks["b"]()
        iu = pidx["U"]; iv = pidx["V"]
        touch(iu); touch(iv)
        WU = ring_view(iu, [128, 8, 512]); WV = ring_view(iv, [128, 8, 512])
        v3 = lambda a_: a_.rearrange("p (h d) -> p h d", d=64)

        def genB(j):
            S_ = SB[j % 2]
            gu, gv, gv2, vn, vnb, ybt, tmpB, st = (S_[k] for k in ("gu", "gv", "gv2", "vn", "vnb", "ybt", "tmpB", "st"))
            tk = slice(j * 128, (j + 1) * 128)
            pu = ps(); pvv = ps()
            for kt in range(8):
                P.mm(pu[:], hT[:, kt, tk], WU[:, kt, :], start=(kt == 0), stop=(kt == 7))
            for kt in range(8):
                P.mm(pvv[:], hT[:, kt, tk], WV[:, kt, :], start=(kt == 0), stop=(kt == 7))
            yield
            P.act(gv, pvv[:], AF.Gelu)
            P.act(gu, pu[:], AF.Gelu)
            yield
            s1 = st[:, 0:8]; s2 = st[:, 8:16]; mean = st[:, 16:24]; mq = st[:, 24:32]; vr = st[:, 32:40]
            P.reduce(s1, v3(gv))
            P.tt(gv2, gv, gv, ALU.mult, eng="pool")
            yield
            P.reduce(s2, v3(gv2))
            P.ts(mean, s1, 1.0 / 64, None, ALU.mult)
            P.tt(mq, mean, mean, ALU.mult)
            P.stt(vr, s2, 1.0 / 64, mq, ALU.mult, ALU.subtract)
            yield
            P.act(vr, vr, AF.Ln, bias=EPS)
            P.act(vr, vr, AF.Exp, scale=-0.5)
            P.tt(v3(vn), v3(gv), bc(mean, [64]), ALU.subtract)
            yield
            P.tt(v3(vn), v3(vn), bc(vr, [64]), ALU.mult)
            yield
            P.tt(vn, vn, lnb[:, 0:512], ALU.mult, eng="pool")
            P.tt(vnb, vn, lnb[:, 512:1024], ALU.add, eng="pool")
            yield
            pmx = ps()
            for h in range(8):
                P.mm(pmx[:, h * 64:(h + 1) * 64], wsT[:, h * 128:(h + 1) * 128], vnb[:, h * 64:(h + 1) * 64])
            yield
            P.tt(v3(tmpB), v3(pmx[:]), bc(cl(C_BST, 8), [64]), ALU.add)
            yield
            P.tt(ybt, tmpB, gu, ALU.mult, eng="pool")
            yield
            pt = ps()[:].bitcast(BF16)
            for c4 in range(4):
                P.transpose(pt[:, c4 * 128:(c4 + 1) * 128], ybt[:, c4 * 128:(c4 + 1) * 128], identb[:])
            yield
            P.add("act", lambda e, o=yT[:, 4:8, tk], i=pt[:, 0:512].rearrange("p (a b) -> p a b", b=128): e.copy(o, i),
                  reads=[pt[:, 0:512]], writes=[yT[:, 4:8, tk]])

        def genX(cts, with_cm):
            for ct in cts:
                ix = pidx["X%d" % (ct // 4)]
                touch(ix, keep=3 + ct // 4)
                WX = ring_view(ix, [128, 8, 512])
                px = ps()
                cc = (ct % 4) * 128
                for kt in range(8):
                    P.mm(px[:], WX[:, kt, cc:cc + 128], hT[:, kt, :], start=(kt == 0), stop=(kt == 7))
                if first:
                    P.memset(xbv(ct, 0, 3), 0.0)
                else:
                    P.copy(xbv(ct, 0, 3), histC[:, l, ct, 0:3], eng="pool")
                yield
                P.copy(xbv(ct, 3, 515), px[:], eng="act")
                P.copy(histC[:, l, ct, 0:3], xbv(ct, 512, 515), eng="pool")
                yield
            if with_cm:
                for i in range(4):
                    ct = 8 + i
                    pc = ps()
                    for k in range(4):
                        P.mm(pc[:], dC[:, (ct * 4 + k) * 128:(ct * 4 + k + 1) * 128], xbv(ct, k, k + 512),
                             start=(k == 0), stop=(k == 3))
                    yield
                    P.act(BCv(i, 0, 512), pc[:], AF.Silu, bias=cl(C_CCB + i))
                    yield

        interleave([genB(0), genB(1), genX(range(0, 6), False)])
        interleave([genB(2), genB(3), genX(range(6, 12), True)])

        if "c" in hooks:
            hooks["c"]()
        if first:
            P.memset(Sst[:, l, :, :], 0.0)
        for g in range(2):
            P.copy(Sbf[:, g, :], Sst[:, l, g, :], eng="pool")
        iz = [pidx["Z0"], pidx["Z1"]]
        hbl = lambda a_, n=16: hb16[:, l, a_:a_ + n]
        x3 = lambda a_: a_.rearrange("p (h d) -> p h d", d=64)
        for j in range(NCH):
            tk = slice(j * 128, (j + 1) * 128)
            c0 = j * 128
            for half in range(2):
                pxs = ps()
                for i in range(4):
                    ct = half * 4 + i
                    o = pxs[:, i * 128:(i + 1) * 128]
                    for k in range(4):
                        P.mm(o, xbv(ct, c0 + k, c0 + k + 128), dC[:, (ct * 4 + k) * 128:(ct * 4 + k + 1) * 128],
                             start=(k == 0), stop=False)
                    P.mm(o, onesb[0:2, 0:128], ccb2[0:2, ct * 128:(ct + 1) * 128], start=False, stop=True)
                P.act(xs[:, half * 512:(half + 1) * 512], pxs[:], AF.Silu)
            pb = ps()
            for i in range(2):
                ct = 8 + i
                o = pb[:, i * 128:(i + 1) * 128]
                for k in range(4):
                    P.mm(o, xbv(ct, c0 + k, c0 + k + 128), dC[:, (ct * 4 + k) * 128:(ct * 4 + k + 1) * 128],
                         start=(k == 0), stop=False)
                P.mm(o, onesb[0:2, 0:128], ccb2[0:2, ct * 128:(ct + 1) * 128], start=False, stop=True)
            P.act(Btok, pb[:, 0:256], AF.Silu)
            pd = ps()
            for kt in range(8):
                P.mm(pd[:, 0:16], hT[:, kt, tk], wdt[:, kt, :], start=(kt == 0), stop=(kt == 7))
            dtv = sm[:, 0:16]; dt = sm[:, 16:32]; acs = sm[:, 48:64]; nacs = sm[:, 64:80]
            ea = sm[:, 80:96]; dec = sm[:, 96:112]; cd = sm[:, 112:128]; dtd = sm[:, 128:144]
            avp = sm[:, 160:208]
            av = avp[:, 0:16]
            P.tt(dtv, pd[:, 0:16], hbl(0), ALU.add)
            P.act(dtv, dtv, AF.Exp)
            P.act(dt, dtv, AF.Ln, bias=1.0)
            P.memset(avp[:, 16:32], 0.0)
            P.tt(av, dt, hbl(16), ALU.mult)
            P.tt(avp[:, 32:48], dt, hbl(16), ALU.mult)
            ab = sm[:, 208:256].bitcast(BF16)
            ahi = ab[:, 0:48]; alo = ab[:, 48:96]
            P.copy(ahi, avp, eng="pool")
            P.tt(alo, avp, ahi, ALU.subtract)
            pa = ps()
            P.mm(pa[:, 0:16], Utrib[:], ahi[:, 0:16], start=True, stop=False)
            P.mm(pa[:, 0:16], Utrib[:], alo[:, 0:16], start=False, stop=True)
            P.mm(pa[:, 16:32], onesbb[:], ahi[:, 0:16], start=True, stop=False)
            P.mm(pa[:, 16:32], onesbb[:], alo[:, 0:16], start=False, stop=True)
            P.mm(pa[0:48, 32:160], ahi, Utrib[:], start=True, stop=False)
            P.mm(pa[0:48, 32:160], alo, Utrib[:], start=False, stop=True)
            P.copy(acs, pa[:, 0:16], eng="dve")
            P.ts(nacs, pa[:, 0:16], -1.0, None, ALU.mult)
            P.act(ea, pa[:, 0:16], AF.Exp)
            P.tt(dec, pa[:, 16:32], acs, ALU.subtract)
            P.act(dec, dec, AF.Exp)
            P.act(cd, pa[:, 16:32], AF.Exp)
            P.copy(acs2[0:48, :], pa[0:48, 32:160], eng="act")
            P.tt(acs2[32:48, :], pa[32:48, 32:160], acs2[32:48, :], ALU.subtract)
            P.tt(dtd, dt, dec, ALU.mult)
            P.tt(x3(xc), x3(xs), bc(dt, [64]), ALU.mult)
            P.tt(x3(xcd), x3(xs), bc(dtd, [64]), ALU.mult, eng="pool")
            for g in range(2):
                touch(iz[g])
                WZ = ring_view(iz[g], [128, 8, 512])
                pz = ps()
                for kt in range(8):
                    P.mm(pz[:], hT[:, kt, tk], WZ[:, kt, :], start=(kt == 0), stop=(kt == 7))
                P.act(sz[g], pz[:], AF.Silu)
            def genG(g):
                y1g = y1[g]; xdg = xd[g]; yctg = yct[g]; CBTg = CBT[g]
                gs = slice(g * 512, (g + 1) * 512)
                bk = ps_banks[4 * g:4 * g + 4]
                pcb = bk[0]
                P.mm(pcb[:, 0:128], BCv(g, c0, c0 + 128), BCv(2 + g, c0, c0 + 128))
                pyo = bk[1]
                P.mm(pyo[:], BCv(2 + g, c0, c0 + 128), Sbf[:, g, :])
                yield
                P.copy(CBTg, pcb[:, 0:128], eng="act")
                P.tt(x3(xdg), x3(xs[:, gs]), bc(hbl(32 + g * 8, 8), [64]), ALU.mult, eng="pool")
                P.tt(x3(y1g), x3(pyo[:]), bc(ea[:, g * 8:(g + 1) * 8], [64]), ALU.mult)
                P.tt(y1g, y1g, xdg, ALU.add, eng="pool")
                pyd = bk[2]
                pLs = [bk[3], bk[0]]
                for quad in range(2):
                    pL = pLs[quad]
                    for q4 in range(4):
                        h = g * 8 + quad * 4 + q4
                        o = pL[:, q4 * 128:(q4 + 1) * 128]
                        P.mm(o, Esel[:, h, :], acs2[0:48, :], start=True, stop=False)
                        P.mm(o, identb[:], maskb[:], start=False, stop=True)
                    yield
                for quad in range(2):
                    pL = pLs[quad]
                    for q4 in range(4):
                        hh = quad * 4 + q4
                        h = g * 8 + hh
                        o = pL[:, q4 * 128:(q4 + 1) * 128]
                        lt = LT[g][hh % 2]; mt = MT[g][hh % 2]
                        P.act(lt, o, AF.Exp, bias=nacs[:, h:h + 1])
                        P.tt(mt, lt, CBTg, ALU.mult, eng=("dve" if hh % 2 == 0 else "pool"))
                        P.mm(pyd[:, hh * 64:(hh + 1) * 64], mt, xc[:, h * 64:(h + 1) * 64])
                    yield
                P.tt(y1g, y1g, pyd[:], ALU.add)
                yield
                P.tt(y1g, y1g, sz[g], ALU.mult)
                yield
                ssq = st8[:, 40 + 2 * g:41 + 2 * g]; sd = st8[:, 41 + 2 * g:42 + 2 * g]
                P.act(xdg, y1g, AF.Square, accum_out=ssq)
                P.act(sd, ssq, AF.Ln, bias=EPS, scale=1.0 / 512)
                P.act(sd, sd, AF.Exp, scale=-0.5)
                yield
                P.ts(yctg, y1g, sd, None, ALU.mult)
                yield
                pt = bk[3][:].bitcast(BF16)
                for c4 in range(4):
                    P.transpose(pt[:, c4 * 128:(c4 + 1) * 128], yctg[:, c4 * 128:(c4 + 1) * 128], identb[:])
                pst = bk[0]
                P.mm(pst[:], Btok[:, g * 128:(g + 1) * 128], xcd[:, gs])
                yield
                yo = yT[:, 8 + g * 4:12 + g * 4, tk]
                P.tt(yo, pt[:, 0:512].rearrange("p (a b) -> p a b", b=128), bc(cl(C_NCG + g * 4, 4), [128]), ALU.mult)
                Sg = Sst[:, l, g, :]
                P.tt(x3(Sg), x3(Sg), bc(cd[:, g * 8:(g + 1) * 8], [64]), ALU.mult, eng="pool")
                yield
                P.tt(Sg, Sg, pst[:], ALU.add)
                yield
                P.copy(Sbf[:, g, :], Sg, eng="act")

            interleave([genG(0), genG(1)])
        if "o" in hooks:
            hooks["o"]()
        po = [ps() for _ in range(8)]
        for q in range(4):
            io = pidx[("O", q)]
            touch(io)
            WO = ring_view(io, [128, 4, 1024])
            for co in range(8):
                for kk in range(4):
                    P.mm(po[co][:], WO[:, kk, co * 128:(co + 1) * 128], yT[:, q * 4 + kk, :],
                         start=(q == 0 and kk == 0), stop=(q == 3 and kk == 3))
        for co in range(8):
            P.tt(xT[:, co, :], xT[:, co, :], po[co][:], ALU.add)
        if "ffn" in hooks:
            hooks["ffn"]()
        rs_ = rmsnorm_to_hT(l, C_G2)
        for kt in range(8):
            P.stt(hT[:, kt, :], xT[:, kt, :], cl(C_G2 + kt), rs_, ALU.mult, ALU.mult)
        for j in range(8):
            i1 = pidx[("F1", j)]
            touch(i1)
            W1 = ring_view(i1, [128, 8, 512])
            for c4 in range(4):
                pf = ps()
                for kt in range(8):
                    P.mm(pf[:], W1[:, kt, c4 * 128:(c4 + 1) * 128], hT[:, kt, :], start=(kt == 0), stop=(kt == 7))
                r_ = rl[c4 % 2]
                P.act(r_, pf[:], AF.Relu)
                ui = j * 4 + c4
                P.tt(uT[:, ui * 512:(ui + 1) * 512], r_, r_, ALU.mult, eng="pool")
        if "ff2" in hooks:
            hooks["ff2"]()
        po = [ps() for _ in range(8)]
        for q in range(8):
            i2 = pidx[("F2", q)]
            touch(i2)
            W2 = ring_view(i2, [128, 4, 1024])
            for co in range(8):
                for kk in range(4):
                    ui = q * 4 + kk
                    P.mm(po[co][:], W2[:, kk, co * 128:(co + 1) * 128], uT[:, ui * 512:(ui + 1) * 512],
                         start=(q == 0 and kk == 0), stop=(q == 7 and kk == 3))
        for co in range(8):
            P.tt(xT[:, co, :], xT[:, co, :], po[co][:], ALU.add)

    stage3 = stage.rearrange("p (c d) -> p c d", d=T)
    ostage3 = scr[:, 4096:8192].rearrange("p (c d) -> p c d", d=T)
    outs = []
    plans = [[[plan_layer(l) for l in range(DEPTH)] for ti in range(NT)] for b in range(NSEQ)]
    tiles = [(b, ti) for b in range(NSEQ) for ti in range(NT)]

    def xsrc(t_ap, b, ti):
        return bass.AP(t_ap.tensor, b * D * S + ti * T, [[S, 128], [128 * S, 8], [1, T]])

    for n, (b, ti) in enumerate(tiles):
        if n == 0:
            P.dma(stage3, xsrc(x_d, b, ti), eng="sp", grp="xin")
        for kt in range(8):
            P.copy(xT[:, kt, :], stage3[:, kt, :], eng=("act", "dve", "pool")[kt % 3])
        for l in range(DEPTH):
            hooks = {}
            if b == 0 and ti == 0 and l + 1 < DEPTH:
                hooks = {ph: (lambda ll=l + 1, bb=bi: prepass(ll, bb)) for bi, ph in enumerate(("a", "b", "c", "o"))}
            if l == DEPTH - 1 and n + 1 < len(tiles):
                nb, nti = tiles[n + 1]
                hooks["ffn"] = (lambda nb=nb, nti=nti: P.dma(stage3, xsrc(x_d, nb, nti), eng="sp", grp="xin"))
            block(l, ti == 0, plans[b][ti][l], hooks)
        rs_ = rmsnorm_to_hT(0, 0)
        for kt in range(8):
            P.stt(ostage3[:, kt, :], xT[:, kt, :], fin[:, kt:kt + 1], rs_, ALU.mult, ALU.mult)
        dst = xsrc(out_d, b, ti)
        P.dma(dst, ostage3, eng="sp", grp="xout")
        outs.append(dst)
    P.add("sp", lambda e: None, reads=outs)
    P.emit()
    return nc, P


def prep_weights(norm1_g, w_in, conv_a_w, conv_a_b, ln_a_g, ln_a_b, ln_b_g, ln_b_b, w_spatial, b_spatial,
                 conv_c_w, conv_c_b, dt_bias, a_log, d_skip, norm_c_g, w_out, norm2_g, w_ff1, w_ff2, final_g):
    L = w_in.shape[0]
    f = np.float32
    perm = []
    for ct in range(4):
        perm += list(range(ct * 128, (ct + 1) * 128)) + list(range(512 + ct * 128, 512 + (ct + 1) * 128))
    perm += list(range(1024, DIN))
    w_in_p = np.ascontiguousarray(np.asarray(w_in, f)[:, :, perm])
    cols = np.zeros((L, 128, NCOLS), f)
    colv = lambda v, n: np.asarray(v, f).reshape(L, n, 128).transpose(0, 2, 1)
    cols[:, :, C_G1:C_G1 + 8] = colv(norm1_g, 8)
    cols[:, :, C_G2:C_G2 + 8] = colv(norm2_g, 8)
    cols[:, :, C_CAB:C_CAB + 4] = colv(conv_a_b, 4)
    cols[:, :, C_LAG:C_LAG + 4] = colv(ln_a_g, 4)
    cols[:, :, C_LAB:C_LAB + 4] = colv(ln_a_b, 4)
    cols[:, :, C_CCB:C_CCB + 4] = colv(np.asarray(conv_c_b, f)[:, 1024:1536], 4)
    cols[:, :, C_BST:C_BST + 8] = np.asarray(b_spatial, f).transpose(0, 2, 1)
    cols[:, :, C_NCG:C_NCG + 8] = colv(norm_c_g, 8)
    caw = np.asarray(conv_a_w, f).reshape(L, 31, 4, 128).transpose(0, 3, 2, 1)
    cols[:, :, C_CAW:C_CAW + 124] = caw.reshape(L, 128, 124)
    ccw = np.asarray(conv_c_w, f).reshape(L, 4, 12, 128).transpose(0, 3, 2, 1)
    cols[:, :, C_CCW:C_CCW + 48] = ccw.reshape(L, 128, 48)
    rows = np.concatenate([np.asarray(ln_b_g, f), np.asarray(ln_b_b, f), np.asarray(conv_c_b, f),
                           np.asarray(dt_bias, f), np.asarray(a_log, f), np.asarray(d_skip, f)], axis=1)
    assert rows.shape[1] == NROWS
    wst = np.ascontiguousarray(np.asarray(w_spatial, f).transpose(0, 3, 1, 2)).reshape(L, 128, 1024)
    fin = np.ascontiguousarray(np.asarray(final_g, f).reshape(8, 128).T)
    return dict(w_in=w_in_p, w_out=np.ascontiguousarray(np.asarray(w_out, f)),
                w_ff1=np.ascontiguousarray(np.asarray(w_ff1, f)), w_ff2=np.ascontiguousarray(np.asarray(w_ff2, f)),
                cols=cols, rows=np.ascontiguousarray(rows), wst=wst, fin=fin)


_CACHE = {}


def run(x, weights, n_cores, trace=False):
    x = np.asarray(x, np.float32)
    B, S, _ = x.shape
    NSEQ = B // n_cores
    DEPTH = weights["w_in"].shape[0]
    key = (NSEQ, S, DEPTH)
    if key not in _CACHE:
        _CACHE[key] = build_nc(NSEQ, S, DEPTH)[0]
    nc = _CACHE[key]
    in_maps = []
    for c in range(n_cores):
        m = dict(weights)
        m["x"] = np.ascontiguousarray(x[c * NSEQ:(c + 1) * NSEQ].transpose(0, 2, 1))
        in_maps.append(m)
    res = run_bass_kernel_spmd(nc, in_maps, core_ids=list(range(n_cores)))
    return np.ascontiguousarray(np.concatenate([r["out"] for r in res.results], axis=0).transpose(0, 2, 1))


def kernel(x, **params):
    weights = prep_weights(**params)
    return run(x, weights, 8)
```

```python
import numpy as np
from contextlib import ExitStack
import concourse.bass as bass
import concourse.mybir as mybir
from concourse.bass_utils import run_bass_kernel_spmd

F32 = mybir.dt.float32
BF16 = mybir.dt.bfloat16
ALU = mybir.AluOpType
AF = mybir.ActivationFunctionType
AX = mybir.AxisListType
DSZ = {F32: 4, BF16: 2}
COMPUTE = ("pe", "act", "dve", "pool")
ALLENG = COMPUTE + ("sp",)
EPS = 1e-5


def _region(ap):
    name = ap.tensor.name
    sz = DSZ.get(ap.dtype, 4)
    dims = ap.ap
    off = ap.offset
    if "dram" in str(ap.space).lower() or "hbm" in str(ap.space).lower():
        ext = 1
        for s, c in dims:
            ext += (c - 1) * abs(s)
        return (name, 0, 1, off * sz, (off + ext) * sz)
    if name.startswith("ps"):
        return (name, 0, 128, 0, 2048)
    pstep = dims[0][0]
    npart = dims[0][1]
    if pstep == 0:
        pstep = 1 << 40
    p0 = off // pstep
    f0 = off % pstep
    ext = 1
    for s, c in dims[1:]:
        ext += (c - 1) * abs(s)
    return (name, p0, p0 + npart, f0 * sz, (f0 + ext) * sz)


def _ovl(a, b):
    return a[1] < b[2] and b[1] < a[2] and a[3] < b[4] and b[3] < a[4]


def _cov(a, b):
    return a[1] <= b[1] and a[2] >= b[2] and a[3] <= b[3] and a[4] >= b[4]


class Op:
    __slots__ = ("eng", "fn", "deps", "idx", "inc", "count", "grp", "waits")


class Prog:
    def __init__(self, nc):
        self.nc = nc
        self.ops = []
        self.W = {}
        self.R = {}
        self.groups = []

    def add(self, eng, fn, reads=(), writes=(), grp=None):
        op = Op()
        op.eng = eng
        op.fn = fn
        op.idx = len(self.ops)
        op.inc = False
        op.count = 0
        op.grp = grp
        op.waits = None
        if grp is not None and grp not in self.groups:
            self.groups.append(grp)
        deps = set()
        ordered = grp is None
        for ap in reads:
            r = _region(ap)
            wl = self.W.get(r[0])
            if wl:
                for e in wl:
                    if _ovl(e[0], r):
                        d = e[1]
                        if ordered and d.grp is None and d.eng == eng and eng == "pe":
                            continue
                        deps.add(d.idx)
            rl = self.R.setdefault(r[0], [])
            if ordered:
                rl[:] = [e for e in rl if not (e[1].grp is None and e[1].eng == eng and _cov(r, e[0]))]
            rl.append((r, op))
        for ap in writes:
            r = _region(ap)
            wl = self.W.setdefault(r[0], [])
            for e in wl:
                if _ovl(e[0], r):
                    d = e[1]
                    if ordered and d.grp is None and d.eng == eng and eng == "pe":
                        continue
                    deps.add(d.idx)
            rl = self.R.setdefault(r[0], [])
            for e in rl:
                if _ovl(e[0], r):
                    d = e[1]
                    if d is op:
                        continue
                    if ordered and d.grp is None and d.eng == eng and eng == "pe":
                        continue
                    deps.add(d.idx)
            wl[:] = [e for e in wl if not _cov(r, e[0])]
            rl[:] = [e for e in rl if not (_cov(r, e[0]) and e[1] is not op)]
            wl.append((r, op))
        best = {}
        keep = set()
        for d in deps:
            dop = self.ops[d]
            if dop.grp is None:
                if d > best.get(dop.eng, -1):
                    best[dop.eng] = d
            else:
                keep.add(d)
        keep.update(best.values())
        op.deps = keep
        self.ops.append(op)
        return op

    def mm(self, out, lhsT, rhs, start=True, stop=True):
        return self.add("pe", lambda e: e.matmul(out, lhsT, rhs, start=start, stop=stop),
                        reads=[lhsT, rhs], writes=[out])

    def transpose(self, out, in_, ident):
        return self.add("pe", lambda e: e.transpose(out, in_, ident), reads=[in_, ident], writes=[out])

    def act(self, out, in_, func, bias=0.0, scale=1.0, accum_out=None):
        reads = [in_]
        if not isinstance(bias, (int, float)):
            reads.append(bias)
        if not isinstance(scale, (int, float)):
            reads.append(scale)
        writes = [out]
        kw = {}
        if accum_out is not None:
            kw["accum_out"] = accum_out
            writes.append(accum_out)
        return self.add("act", lambda e: e.activation(out, in_, func, bias=bias, scale=scale, **kw),
                        reads=reads, writes=writes)

    def tt(self, out, in0, in1, op, eng="dve"):
        return self.add(eng, lambda e: e.tensor_tensor(out, in0, in1, op), reads=[in0, in1], writes=[out])

    def ts(self, out, in0, s1, s2, op0, op1=None, eng="dve"):
        reads = [in0]
        if not isinstance(s1, (int, float)):
            reads.append(s1)
        if s2 is not None and not isinstance(s2, (int, float)):
            reads.append(s2)
        kw = {}
        if op1 is not None:
            kw["op1"] = op1
        return self.add(eng, lambda e: e.tensor_scalar(out, in0, s1, s2, op0, **kw), reads=reads, writes=[out])

    def stt(self, out, in0, scalar, in1, op0, op1, eng="dve"):
        reads = [in0, in1]
        if not isinstance(scalar, (int, float)):
            reads.append(scalar)
        return self.add(eng, lambda e: e.scalar_tensor_tensor(out, in0, scalar, in1, op0, op1),
                        reads=reads, writes=[out])

    def copy(self, out, in_, eng="dve"):
        if eng == "act":
            return self.add(eng, lambda e: e.copy(out, in_), reads=[in_], writes=[out])
        return self.add(eng, lambda e: e.tensor_copy(out, in_), reads=[in_], writes=[out])

    def recip(self, out, in_):
        return self.add("dve", lambda e: e.reciprocal(out, in_), reads=[in_], writes=[out])

    def reduce(self, out, in_, eng="dve"):
        return self.add(eng, lambda e: e.tensor_reduce(out, in_, AX.X, ALU.add), reads=[in_], writes=[out])

    def memset(self, ap, val, eng="pool"):
        return self.add(eng, lambda e: e.memset(ap, val), reads=[], writes=[ap])

    def asel(self, ap, pattern, cmp, fill, base=0, cm=0):
        return self.add("pool", lambda e: e.affine_select(ap, ap, pattern, cmp, fill, base=base,
                                                          channel_multiplier=cm), reads=[ap], writes=[ap])

    def dma(self, out, in_, eng="sp", grp="g0"):
        return self.add(eng, lambda e: e.dma_start(out, in_), reads=[in_], writes=[out], grp=grp)

    def finalize(self):
        ops = self.ops
        for op in ops:
            for d in op.deps:
                if ops[d].grp is None:
                    ops[d].inc = True
        cnt = {e: 0 for e in ALLENG}
        tot = {g: 0 for g in self.groups}
        seen = {e: {} for e in ALLENG}
        know = {e: {} for e in ALLENG}
        nw = 0
        for op in ops:
            w = {}
            for d in op.deps:
                dop = ops[d]
                if dop.grp is None:
                    key = ("e", dop.eng)
                    val = dop.count
                else:
                    key = ("g", dop.grp)
                    val = tot[dop.grp]
                if val > w.get(key, 0):
                    w[key] = val
            s = seen[op.eng]
            op.waits = []
            for key, val in sorted(w.items(), key=lambda kv: -kv[1]):
                if val > s.get(key, 0):
                    s[key] = val
                    op.waits.append((key, val))
                    nw += 1
                    if key[0] == "e":
                        kd = know[key[1]].get(val)
                        if kd:
                            for k2, v2 in kd.items():
                                if v2 > s.get(k2, 0):
                                    s[k2] = v2
            if op.grp is None:
                if op.inc:
                    cnt[op.eng] += 1
                    kd = dict(s)
                    kd[("e", op.eng)] = cnt[op.eng]
                    know[op.eng][cnt[op.eng]] = kd
                op.count = cnt[op.eng]
            else:
                tot[op.grp] += 16
        self.stats = dict(n_ops=len(ops), n_waits=nw, cnt=cnt, ngroups=len(self.groups))

    def emit(self):
        nc = self.nc
        self.finalize()
        with ExitStack() as es:
            sems = {}
            for e in ALLENG:
                sems[("e", e)] = es.enter_context(nc.semaphore("s_" + e))
            for g in self.groups:
                sems[("g", g)] = es.enter_context(nc.semaphore("g_" + str(g)))
            block = es.enter_context(nc.Block())
            by_eng = {e: [] for e in ALLENG}
            for op in self.ops:
                by_eng[op.eng].append(op)

            def run(engobj, name):
                for op in by_eng[name]:
                    for key, val in op.waits:
                        engobj.wait_ge(sems[key], val)
                    ins = op.fn(engobj)
                    if ins is None:
                        continue
                    if op.grp is not None:
                        ins.then_inc(sems[("g", op.grp)], 16)
                    elif op.inc:
                        ins.then_inc(sems[("e", name)], 1)

            @block.sync
            def _(e):
                run(e, "sp")

            @block.tensor
            def _(e):
                run(e, "pe")

            @block.scalar
            def _(e):
                run(e, "act")

            @block.vector
            def _(e):
                run(e, "dve")

            @block.gpsimd
            def _(e):
                run(e, "pool")


def bc(ap, counts):
    dims = [list(d) for d in ap.ap]
    for c in counts:
        dims.append([0, c])
    return bass.AP(ap.tensor, ap.offset, dims)


D = 1024
DIN = 4624
NCOLS = 220
C_G1, C_G2, C_CAB, C_LAG, C_LAB, C_CCB, C_BST, C_NCG, C_CAW, C_CCW = 0, 8, 16, 20, 24, 28, 32, 40, 48, 172
NROWS = 512 + 512 + 1536 + 48
R_LBG, R_LBB, R_CCB, R_DTB, R_ALOG, R_DSK = 0, 512, 1024, 2560, 2576, 2592
T = 512
NCH = 4
NSLOT = 6
LOOKAHEAD = 4


def build_nc(NSEQ, S, DEPTH):
    nc = bass.Bass("TRN2", target_bir_lowering=False)
    P = Prog(nc)
    NT = S // T

    def din(name, shape, dt=F32):
        return nc.dram_tensor(name, shape, dt, kind="ExternalInput").ap()

    x_d = din("x", [NSEQ, D, S])
    w_in_d = din("w_in", [DEPTH, D, DIN])
    w_out_d = din("w_out", [DEPTH, 2048, D])
    w_ff1_d = din("w_ff1", [DEPTH, D, 4096])
    w_ff2_d = din("w_ff2", [DEPTH, 4096, D])
    cols_d = din("cols", [DEPTH, 128, NCOLS])
    rows_d = din("rows", [DEPTH, NROWS])
    wst_d = din("wst", [DEPTH, 128, 1024])
    fin_d = din("fin", [128, 8])
    out_d = nc.dram_tensor("out", [NSEQ, D, S], F32, kind="ExternalOutput").ap()
    diagA_d = nc.dram_tensor("diagA", [DEPTH, 4, 128, 31 * 128], BF16).ap()
    diagC_d = nc.dram_tensor("diagC", [DEPTH, 128, 48 * 128], BF16).ap()
    NPAN = 31
    wbf_d = nc.dram_tensor("wbf", [DEPTH, NPAN, 128, 4096], BF16).ap()
    wdtb_d = nc.dram_tensor("wdtb", [DEPTH, 128, 128], BF16).ap()

    def wsrc(t_ap, base, rowlen, nk, ncol, col0):
        return bass.AP(t_ap.tensor, base + col0, [[rowlen, 128], [128 * rowlen, nk], [1, ncol]])

    def panel_specs(l):
        b_in = l * D * DIN
        sp_ = []
        for ct in range(4):
            sp_.append((("A", ct), [128, 8, 256], wsrc(w_in_d, b_in, DIN, 8, 256, ct * 256)))
        for nm, c0 in (("U", 1024), ("V", 1536), ("X0", 3072), ("X1", 3584), ("X2", 4096), ("Z0", 2048), ("Z1", 2560)):
            sp_.append((nm, [128, 8, 512], wsrc(w_in_d, b_in, DIN, 8, 512, c0)))
        for q in range(4):
            sp_.append((("O", q), [128, 4, 1024], wsrc(w_out_d, l * 2048 * D + q * 4 * 128 * D, D, 4, 1024, 0)))
        for j in range(8):
            sp_.append((("F1", j), [128, 8, 512], wsrc(w_ff1_d, l * D * 4096, 4096, 8, 512, j * 512)))
        for q in range(8):
            sp_.append((("F2", q), [128, 4, 1024], wsrc(w_ff2_d, l * 4096 * D + q * 4 * 128 * D, D, 4, 1024, 0)))
        assert len(sp_) == NPAN
        return sp_

    ppdummy = nc.dram_tensor("ppdummy", [8, 16], F32).ap()
    ppi = [0]

    def prepass(l, batch=None):
        def one(dst, src):
            gi = ppi[0] % 8
            ppi[0] += 1
            P.add("pool", lambda e: e.dma_start(dst, src), reads=[src], writes=[dst, ppdummy[gi:gi + 1, :]],
                  grp="pp%d_%d" % (l, gi))
        jobs = []
        for pi, (nm, shape, src) in enumerate(panel_specs(l)):
            n = shape[1] * shape[2]
            jobs.append((wbf_d[l, pi][:, 0:n].rearrange("p (a b) -> p a b", b=shape[2]), src))
        jobs.append((wdtb_d[l].rearrange("p (a b) -> p a b", b=16), wsrc(w_in_d, l * D * DIN, DIN, 8, 16, 4608)))
        for i, (dst, src) in enumerate(jobs):
            if batch is None or i // 8 == batch:
                one(dst, src)

    def sb(name, shape, dt=F32):
        return nc.alloc_sbuf_tensor(name, shape, dt)

    xT = sb("xT", [128, 8, T])
    hT = sb("hT", [128, 8, T], BF16)
    ybuf = sb("ybuf", [128, 16 * T], BF16)
    yT = ybuf[:].rearrange("p (a b) -> p a b", b=T)
    ring = sb("ring", [128, NSLOT, 4096], BF16)
    wdt = sb("wdt", [128, 8, 16], BF16)
    dC = sb("dC", [128, 48 * 128], BF16)
    Sst = sb("Sst", [128, DEPTH, 2, 512])
    Sbf = sb("Sbf", [128, 2, 512], BF16)
    histA = sb("histA", [128, DEPTH, 4, 30], BF16)
    histC = sb("histC", [128, DEPTH, 12, 4], BF16)
    ident = sb("ident", [128, 128])
    identb = sb("identb", [128, 128], BF16)
    ones = sb("ones", [128, 128])
    Bd64 = sb("Bd64", [128, 128])
    Utri = sb("Utri", [128, 128])
    maskneg = sb("maskneg", [128, 128])
    Esel = sb("Esel", [48, 16, 128], BF16)
    maskb = sb("maskb", [128, 128], BF16)
    Bd64b = sb("Bd64b", [128, 128], BF16)
    onesbb = sb("onesbb", [128, 128], BF16)
    onesb = sb("onesb", [2, 128], BF16)
    acs2 = sb("acs2", [48, 128], BF16)
    wsT = sb("wsT", [128, 1024], BF16)
    cols = sb("cols_sb", [128, DEPTH, NCOLS])
    fin = sb("fin_sb", [128, 8])
    hb16 = sb("hb16", [128, DEPTH, 48])
    lnb = sb("lnb", [128, 1024])
    ccb2 = sb("ccb2", [2, 1536], BF16)
    ccb_d = nc.dram_tensor("ccbd", [DEPTH, 2, 1536], BF16).ap()
    SCRW = 13056
    scr = sb("scr", [128, SCRW])
    wsm_d = nc.dram_tensor("wsm", [DEPTH, 128, 1024], BF16).ap()
    ps_banks = [nc.alloc_psum_tensor("ps%d" % i, [128, 512], F32) for i in range(8)]
    psi = [0]

    def ps():
        b = ps_banks[psi[0] % 8]
        psi[0] += 1
        return b

    soff = [0]

    def carve(nwords, dt=F32, shape=None):
        a = scr[:, soff[0]:soff[0] + nwords]
        soff[0] += nwords
        assert soff[0] <= SCRW, soff[0]
        if dt == BF16:
            a = a.bitcast(BF16)
        return a

    sqA = carve(512); sqB = carve(512); rs = carve(512)
    st8 = carve(64)
    base = soff[0]
    SA = []
    for _ in range(2):
        SA.append(dict(sg=carve(512), hb=carve(272, BF16), cA=carve(512), cA2=carve(512), cAh=carve(256, BF16), msq=carve(512),
                       var=carve(512), t1=carve(512)))
    soff[0] = base
    xbuf = carve(12 * 258, BF16)
    BCcm = carve(4 * 256, BF16)
    xs = carve(1024)
    xc = carve(512, BF16); xcd = carve(512, BF16)
    Btok = carve(128, BF16)
    sm = carve(256)
    CBT = [carve(128), carve(128)]
    LT = [[carve(128), carve(128)], [carve(128), carve(128)]]
    MT = [[carve(64, BF16), carve(64, BF16)], [carve(64, BF16), carve(64, BF16)]]
    y1 = [carve(512), carve(512)]; xd = [carve(512), carve(512)]; sz = [carve(512), carve(512)]
    yct = [carve(256, BF16), carve(256, BF16)]
    c_end = soff[0]
    soff[0] = c_end - 2 * 3136
    assert soff[0] >= base + 12 * 258 + 4 * 256
    SB = []
    for _ in range(2):
        SB.append(dict(gu=carve(512), gv=carve(512), gv2=carve(512), vn=carve(512), vnb=carve(256, BF16),
                       ybt=carve(256, BF16), tmpB=carve(512), st=carve(64)))
    assert soff[0] == c_end
    uT = scr[:, 0:8192].bitcast(BF16)
    rl = [scr[:, 8192:8704], scr[:, 8704:9216]]
    stage = ybuf[:].bitcast(F32)

    xbv = lambda ct, a, b: xbuf[:, ct * 516 + a: ct * 516 + b]
    BCv = lambda i, a, b: BCcm[:, i * 512 + a: i * 512 + b]

    P.memset(ident[:], 0.0)
    P.asel(ident[:], [[-1, 128]], ALU.not_equal, 1.0, base=0, cm=1)
    P.copy(identb[:], ident[:], eng="dve")
    P.memset(ones[:], 1.0)
    P.memset(Bd64[:], 0.0)
    P.memset(Bd64[0:64, 0:64], 1.0 / 64)
    P.memset(Bd64[64:128, 64:128], 1.0 / 64)
    P.memset(Utri[:], 1.0)
    P.asel(Utri[:], [[1, 128]], ALU.is_ge, 0.0, base=0, cm=-1)
    P.memset(maskneg[:], 0.0)
    P.asel(maskneg[:], [[1, 128]], ALU.is_ge, -30000.0, base=0, cm=-1)
    P.memset(Esel[:], 0.0)
    P.asel(Esel[0:16], [[-1, 16], [0, 128]], ALU.not_equal, 1.0, base=0, cm=1)
    P.dma(Esel[32:48], Esel[0:16], eng="sp", grp="c3")
    P.copy(maskb[:], maskneg[:], eng="dve")
    P.copy(Bd64b[:], Bd64[:], eng="dve")
    P.memset(onesbb[:], 1.0)
    P.memset(onesb[:], 1.0)
    P.memset(acs2[:], 0.0)
    P.dma(fin[:], fin_d, eng="sp", grp="c0")
    for l in range(DEPTH):
        P.dma(wsT[:], wst_d[l], eng="pool", grp="c1")
        P.asel(wsT[:].rearrange("p (h t) -> p h t", t=128), [[0, 8], [1, 128]], ALU.is_ge, 0.0, base=0, cm=-1)
        P.dma(wsm_d[l], wsT[:], eng="sp", grp="c2")
    prepass(0)
    dtmp = [scr[:, 0:1984].bitcast(BF16), scr[:, 1984:3968].bitcast(BF16)]
    dtmpC = scr[:, 3968:3968 + 3072].bitcast(BF16)
    for l in range(DEPTH):
        P.dma(cols[:, l, :], cols_d[l], eng="sp", grp="c0")
        P.dma(hb16[:, l, :], bass.AP(rows_d.tensor, l * NROWS + R_DTB, [[0, 128], [1, 48]]), eng="sp", grp="c0")
        btmp = scr[0:1, 8000:8000 + 1536]
        bhi = scr[0:1, 9600:9600 + 768].bitcast(BF16)
        blo = scr[0:1, 10400:10400 + 768].bitcast(BF16)
        P.dma(btmp, rows_d[l:l + 1, R_CCB:R_CCB + 1536], eng="sp", grp="c0")
        P.copy(bhi, btmp, eng="act")
        P.tt(blo, btmp, bhi, ALU.subtract)
        P.dma(ccb_d[l, 0:1, :], bhi, eng="sp", grp="c2")
        P.dma(ccb_d[l, 1:2, :], blo, eng="sp", grp="c2")
        P.act(hb16[:, l, 16:32], hb16[:, l, 16:32], AF.Exp)
        P.ts(hb16[:, l, 16:32], hb16[:, l, 16:32], -1.0, None, ALU.mult)
        for ct in range(4):
            dt_ = dtmp[ct % 2]
            for k in range(31):
                c = C_CAW + ct * 31 + k
                if k % 2 == 0:
                    P.act(dt_[:, k * 128:(k + 1) * 128], identb[:], AF.Copy, scale=cols[:, l, c:c + 1])
                else:
                    P.ts(dt_[:, k * 128:(k + 1) * 128], identb[:], cols[:, l, c:c + 1], None, ALU.mult)
            P.dma(diagA_d[l, ct], dt_[:, 0:3968], eng="sp", grp="c2")
        for ct in range(12):
            for k in range(4):
                c = C_CCW + ct * 4 + k
                i = ct * 4 + k
                if i % 2 == 0:
                    P.act(dtmpC[:, i * 128:(i + 1) * 128], identb[:], AF.Copy, scale=cols[:, l, c:c + 1])
                else:
                    P.ts(dtmpC[:, i * 128:(i + 1) * 128], identb[:], cols[:, l, c:c + 1], None, ALU.mult)
        P.dma(diagC_d[l], dtmpC[:, 0:6144], eng="sp", grp="c2")

    reqs = []
    issued = [0]

    def ring_view(i, shape):
        slot = i % NSLOT
        a = ring[:, slot, :]
        if len(shape) == 2:
            return a[:, 0:shape[1]]
        return a[:, 0:shape[1] * shape[2]].rearrange("p (a b) -> p a b", b=shape[2])

    def touch(i, keep=2):
        while issued[0] <= min(i + NSLOT - keep, len(reqs) - 1):
            j = issued[0]
            n, src, eng = reqs[j]
            P.dma(ring[:, j % NSLOT, 0:n], src, eng=eng, grp=("w%d%s" % (j % NSLOT, eng)))
            issued[0] += 1

    def plan_layer(l):
        idx = {}
        specs = {nm: (pi, shape) for pi, (nm, shape, src) in enumerate(panel_specs(l))}

        def req(nm):
            pi, shape = specs[nm]
            n = shape[1] * shape[2]
            idx[nm] = len(reqs)
            reqs.append((n, wbf_d[l, pi][:, 0:n], "sp"))

        def reqd(ct):
            idx[("dA", ct)] = len(reqs)
            reqs.append((3968, diagA_d[l, ct], "sp"))

        for pair in range(2):
            req(("A", 2 * pair)); req(("A", 2 * pair + 1)); reqd(2 * pair); reqd(2 * pair + 1)
        for nm in ("U", "V", "X0", "X1", "X2", "Z0", "Z1"):
            req(nm)
        for q in range(4):
            req(("O", q))
        for j in range(8):
            req(("F1", j))
        for q in range(8):
            req(("F2", q))
        return idx

    def interleave(gens):
        gens = list(gens)
        while gens:
            nxt = []
            for g_ in gens:
                try:
                    next(g_)
                    nxt.append(g_)
                except StopIteration:
                    pass
            gens = nxt

    def rmsnorm_to_hT(l, gcol, gsrc=None):
        pss = ps()
        for kt in range(8):
            sq = (sqA if kt % 2 == 0 else sqB).bitcast(BF16)[:, 0:512]
            if kt % 2 == 0:
                P.act(sq, xT[:, kt, :], AF.Square)
            else:
                P.tt(sq, xT[:, kt, :], xT[:, kt, :], ALU.mult, eng="pool")
            P.mm(pss[:], onesbb[:], sq, start=(kt == 0), stop=(kt == 7))
        P.act(rs, pss[:], AF.Ln, bias=EPS, scale=1.0 / D)
        P.act(rs, rs, AF.Exp, scale=-0.5)
        return rs

    def block(l, first, pidx, hooks=None):
        hooks = hooks or {}
        cl = lambda c0, n=1: cols[:, l, c0:c0 + n]
        P.dma(lnb[:], bass.AP(rows_d.tensor, l * NROWS + R_LBG, [[0, 128], [1, 1024]]), eng="sp", grp="r0")
        P.dma(ccb2[:], ccb_d[l], eng="sp", grp="r1")
        P.dma(dC[:], diagC_d[l], eng="sp", grp="r2")
        P.dma(wsT[:], wsm_d[l], eng="sp", grp="r4")
        P.dma(wdt[:], wdtb_d[l].rearrange("p (a b) -> p a b", b=16), eng="sp", grp="r3")
        rs_ = rmsnorm_to_hT(l, C_G1)
        for kt in range(8):
            P.stt(hT[:, kt, :], xT[:, kt, :], cl(C_G1 + kt), rs_, ALU.mult, ALU.mult)
        if "a" in hooks:
            hooks["a"]()
        def genA(ct):
            S_ = SA[ct % 2]
            sg, hb, cA, cA2, msq, var, t1 = (S_[k] for k in ("sg", "hb", "cA", "cA2", "msq", "var", "t1"))
            ia = pidx[("A", ct)]
            touch(ia, keep=3)
            WA = ring_view(ia, [128, 8, 256])
            pv = ps(); pg = ps()
            for kt in range(8):
                P.mm(pv[:], WA[:, kt, 0:128], hT[:, kt, :], start=(kt == 0), stop=(kt == 7))
            for kt in range(8):
                P.mm(pg[:], WA[:, kt, 128:256], hT[:, kt, :], start=(kt == 0), stop=(kt == 7))
            yield
            P.act(sg, pg[:], AF.Sigmoid)
            if first:
                P.memset(hb[:, 0:30], 0.0)
            else:
                P.copy(hb[:, 0:30], histA[:, l, ct, :], eng="pool")
            P.tt(hb[:, 30:542], pv[:], sg, ALU.mult)
            P.copy(histA[:, l, ct, :], hb[:, 512:542], eng="pool")
            ida = pidx[("dA", ct)]
            touch(ida, keep=3)
            dA = ring_view(ida, [128, 3968])
            pc = ps()
            for k in range(31):
                P.mm(pc[:], dA[:, k * 128:(k + 1) * 128], hb[:, k:k + 512], start=(k == 0), stop=(k == 30))
            yield
            cAh = S_["cAh"]
            cA2b = cA2.bitcast(BF16)[:, 0:512]
            P.act(cA, pc[:], AF.Identity, bias=cl(C_CAB + ct))
            P.act(cA2b, pc[:], AF.Square, bias=cl(C_CAB + ct))
            P.copy(cAh, cA, eng="pool")
            pm = ps(); pe = ps()
            P.mm(pm[:], Bd64b[:], cAh)
            P.mm(pe[:], Bd64b[:], cA2b)
            yield
            P.act(msq, pm[:], AF.Square)
            P.tt(var, pe[:], msq, ALU.subtract)
            P.tt(t1, cA, pm[:], ALU.subtract)
            yield
            P.act(var, var, AF.Ln, bias=EPS)
            P.act(var, var, AF.Exp, scale=-0.5)
            yield
            P.tt(t1, t1, var, ALU.mult, eng="pool")
            yield
            P.act(yT[:, ct, :], t1, AF.Silu, bias=cl(C_LAB + ct), scale=cl(C_LAG + ct))

        for pair in range(2):
            interleave([genA(2 * pair), genA(2 * pair + 1)])

        if "b" in hooks:
            hooks["b"]()
        iu = pidx["U"]; iv = pidx["V"]
        touch(iu); touch(iv)
        WU = ring_view(iu, [128, 8, 512]); WV = ring_view(iv, [128, 8, 512])
        v3 = lambda a_: a_.rearrange("p (h d) -> p h d", d=64)

        def genB(j):
            S_ = SB[j % 2]
            gu, gv, gv2, vn, vnb, ybt, tmpB, st = (S_[k] for k in ("gu", "gv", "gv2", "vn", "vnb", "ybt", "tmpB", "st"))
            tk = slice(j * 128, (j + 1) * 128)
            pu = ps(); pvv = ps()
            for kt in range(8):
                P.mm(pu[:], hT[:, kt, tk], WU[:, kt, :], start=(kt == 0), stop=(kt == 7))
            for kt in range(8):
                P.mm(pvv[:], hT[:, kt, tk], WV[:, kt, :], start=(kt == 0), stop=(kt == 7))
            yield
            P.act(gv, pvv[:], AF.Gelu)
            P.act(gu, pu[:], AF.Gelu)
            yield
            s1 = st[:, 0:8]; s2 = st[:, 8:16]; mean = st[:, 16:24]; mq = st[:, 24:32]; vr = st[:, 32:40]
            P.reduce(s1, v3(gv))
            P.tt(gv2, gv, gv, ALU.mult, eng="pool")
            yield
            P.reduce(s2, v3(gv2))
            P.ts(mean, s1, 1.0 / 64, None, ALU.mult)
            P.tt(mq, mean, mean, ALU.mult)
            P.stt(vr, s2, 1.0 / 64, mq, ALU.mult, ALU.subtract)
            yield
            P.act(vr, vr, AF.Ln, bias=EPS)
            P.act(vr, vr, AF.Exp, scale=-0.5)
            P.tt(v3(vn), v3(gv), bc(mean, [64]), ALU.subtract)
            yield
            P.tt(v3(vn), v3(vn), bc(vr, [64]), ALU.mult)
            yield
            P.tt(vn, vn, lnb[:, 0:512], ALU.mult)
            P.tt(vnb, vn, lnb[:, 512:1024], ALU.add, eng="pool")
            yield
            pmx = ps()
            for h in range(8):
                P.mm(pmx[:, h * 64:(h + 1) * 64], wsT[:, h * 128:(h + 1) * 128], vnb[:, h * 64:(h + 1) * 64])
            yield
            P.tt(v3(tmpB), v3(pmx[:]), bc(cl(C_BST, 8), [64]), ALU.add)
            yield
            P.tt(ybt, tmpB, gu, ALU.mult, eng="pool")
            yield
            pt = ps()[:].bitcast(BF16)
            for c4 in range(4):
                P.transpose(pt[:, c4 * 128:(c4 + 1) * 128], ybt[:, c4 * 128:(c4 + 1) * 128], identb[:])
            yield
            P.add("act", lambda e, o=yT[:, 4:8, tk], i=pt[:, 0:512].rearrange("p (a b) -> p a b", b=128): e.copy(o, i),
                  reads=[pt[:, 0:512]], writes=[yT[:, 4:8, tk]])

        def genX(cts, with_cm):
            for ct in cts:
                ix = pidx["X%d" % (ct // 4)]
                touch(ix, keep=3 + ct // 4)
                WX = ring_view(ix, [128, 8, 512])
                px = ps()
                cc = (ct % 4) * 128
                for kt in range(8):
                    P.mm(px[:], WX[:, kt, cc:cc + 128], hT[:, kt, :], start=(kt == 0), stop=(kt == 7))
                if first:
                    P.memset(xbv(ct, 0, 3), 0.0)
                else:
                    P.copy(xbv(ct, 0, 3), histC[:, l, ct, 0:3], eng="pool")
                yield
                P.copy(xbv(ct, 3, 515), px[:], eng="act")
                P.copy(histC[:, l, ct, 0:3], xbv(ct, 512, 515), eng="pool")
                yield
            if with_cm:
                for i in range(4):
                    ct = 8 + i
                    pc = ps()
                    for k in range(4):
                        P.mm(pc[:], dC[:, (ct * 4 + k) * 128:(ct * 4 + k + 1) * 128], xbv(ct, k, k + 512),
                             start=(k == 0), stop=(k == 3))
                    yield
                    P.act(BCv(i, 0, 512), pc[:], AF.Silu, bias=cl(C_CCB + i))
                    yield

        interleave([genB(0), genB(1), genX(range(0, 6), False)])
        interleave([genB(2), genB(3), genX(range(6, 12), True)])

        if "c" in hooks:
            hooks["c"]()
        if first:
            P.memset(Sst[:, l, :, :], 0.0)
        for g in range(2):
            P.copy(Sbf[:, g, :], Sst[:, l, g, :], eng="pool")
        iz = [pidx["Z0"], pidx["Z1"]]
        hbl = lambda a_, n=16: hb16[:, l, a_:a_ + n]
        x3 = lambda a_: a_.rearrange("p (h d) -> p h d", d=64)
        for j in range(NCH):
            tk = slice(j * 128, (j + 1) * 128)
            c0 = j * 128
            for half in range(2):
                pxs = ps()
                for i in range(4):
                    ct = half * 4 + i
                    o = pxs[:, i * 128:(i + 1) * 128]
                    for k in range(4):
                        P.mm(o, xbv(ct, c0 + k, c0 + k + 128), dC[:, (ct * 4 + k) * 128:(ct * 4 + k + 1) * 128],
                             start=(k == 0), stop=False)
                    P.mm(o, onesb[0:2, 0:128], ccb2[0:2, ct * 128:(ct + 1) * 128], start=False, stop=True)
                P.act(xs[:, half * 512:(half + 1) * 512], pxs[:], AF.Silu)
            pb = ps()
            for i in range(2):
                ct = 8 + i
                o = pb[:, i * 128:(i + 1) * 128]
                for k in range(4):
                    P.mm(o, xbv(ct, c0 + k, c0 + k + 128), dC[:, (ct * 4 + k) * 128:(ct * 4 + k + 1) * 128],
                         start=(k == 0), stop=False)
                P.mm(o, onesb[0:2, 0:128], ccb2[0:2, ct * 128:(ct + 1) * 128], start=False, stop=True)
            P.act(Btok, pb[:, 0:256], AF.Silu)
            pd = ps()
            for kt in range(8):
                P.mm(pd[:, 0:16], hT[:, kt, tk], wdt[:, kt, :], start=(kt == 0), stop=(kt == 7))
            dtv = sm[:, 0:16]; dt = sm[:, 16:32]; acs = sm[:, 48:64]; nacs = sm[:, 64:80]
            ea = sm[:, 80:96]; dec = sm[:, 96:112]; cd = sm[:, 112:128]; dtd = sm[:, 128:144]
            avp = sm[:, 160:208]
            av = avp[:, 0:16]
            P.tt(dtv, pd[:, 0:16], hbl(0), ALU.add)
            P.act(dtv, dtv, AF.Exp)
            P.act(dt, dtv, AF.Ln, bias=1.0)
            P.memset(avp[:, 16:32], 0.0)
            P.tt(av, dt, hbl(16), ALU.mult)
            P.tt(avp[:, 32:48], dt, hbl(16), ALU.mult)
            pa = ps()
            P.mm(pa[:, 0:16], Utri[:], av)
            P.mm(pa[:, 16:32], ones[:], av)
            P.mm(pa[0:48, 32:160], avp, Utri[:])
            P.copy(acs, pa[:, 0:16], eng="dve")
            P.ts(nacs, pa[:, 0:16], -1.0, None, ALU.mult)
            P.act(ea, pa[:, 0:16], AF.Exp)
            P.tt(dec, pa[:, 16:32], acs, ALU.subtract)
            P.act(dec, dec, AF.Exp)
            P.act(cd, pa[:, 16:32], AF.Exp)
            P.copy(acs2[0:48, :], pa[0:48, 32:160], eng="act")
            P.tt(acs2[32:48, :], pa[32:48, 32:160], acs2[32:48, :], ALU.subtract)
            P.tt(dtd, dt, dec, ALU.mult)
            P.tt(x3(xc), x3(xs), bc(dt, [64]), ALU.mult)
            P.tt(x3(xcd), x3(xs), bc(dtd, [64]), ALU.mult, eng="pool")
            for g in range(2):
                touch(iz[g])
                WZ = ring_view(iz[g], [128, 8, 512])
                pz = ps()
                for kt in range(8):
                    P.mm(pz[:], hT[:, kt, tk], WZ[:, kt, :], start=(kt == 0), stop=(kt == 7))
                P.act(sz[g], pz[:], AF.Silu)
            def genG(g):
                y1g = y1[g]; xdg = xd[g]; yctg = yct[g]; CBTg = CBT[g]
                gs = slice(g * 512, (g + 1) * 512)
                bk = ps_banks[4 * g:4 * g + 4]
                pcb = bk[0]
                P.mm(pcb[:, 0:128], BCv(g, c0, c0 + 128), BCv(2 + g, c0, c0 + 128))
                pyo = bk[1]
                P.mm(pyo[:], BCv(2 + g, c0, c0 + 128), Sbf[:, g, :])
                yield
                P.copy(CBTg, pcb[:, 0:128], eng="act")
                P.tt(x3(xdg), x3(xs[:, gs]), bc(hbl(32 + g * 8, 8), [64]), ALU.mult, eng="pool")
                P.tt(x3(y1g), x3(pyo[:]), bc(ea[:, g * 8:(g + 1) * 8], [64]), ALU.mult)
                P.tt(y1g, y1g, xdg, ALU.add, eng="pool")
                pyd = bk[2]
                pLs = [bk[3], bk[0]]
                for quad in range(2):
                    pL = pLs[quad]
                    for q4 in range(4):
                        h = g * 8 + quad * 4 + q4
                        o = pL[:, q4 * 128:(q4 + 1) * 128]
                        P.mm(o, Esel[:, h, :], acs2[0:48, :], start=True, stop=False)
                        P.mm(o, identb[:], maskb[:], start=False, stop=True)
                    yield
                for quad in range(2):
                    pL = pLs[quad]
                    for q4 in range(4):
                        hh = quad * 4 + q4
                        h = g * 8 + hh
                        o = pL[:, q4 * 128:(q4 + 1) * 128]
                        lt = LT[g][hh % 2]; mt = MT[g][hh % 2]
                        P.act(lt, o, AF.Exp, bias=nacs[:, h:h + 1])
                        P.tt(mt, lt, CBTg, ALU.mult, eng=("dve" if hh % 2 == 0 else "pool"))
                        P.mm(pyd[:, hh * 64:(hh + 1) * 64], mt, xc[:, h * 64:(h + 1) * 64])
                    yield
                P.tt(y1g, y1g, pyd[:], ALU.add)
                yield
                P.tt(y1g, y1g, sz[g], ALU.mult)
                yield
                ssq = st8[:, 40 + 2 * g:41 + 2 * g]; sd = st8[:, 41 + 2 * g:42 + 2 * g]
                P.act(xdg, y1g, AF.Square, accum_out=ssq)
                P.act(sd, ssq, AF.Ln, bias=EPS, scale=1.0 / 512)
                P.act(sd, sd, AF.Exp, scale=-0.5)
                yield
                P.ts(yctg, y1g, sd, None, ALU.mult)
                yield
                pt = bk[3][:].bitcast(BF16)
                for c4 in range(4):
                    P.transpose(pt[:, c4 * 128:(c4 + 1) * 128], yctg[:, c4 * 128:(c4 + 1) * 128], identb[:])
                pst = bk[0]
                P.mm(pst[:], Btok[:, g * 128:(g + 1) * 128], xcd[:, gs])
                yield
                yo = yT[:, 8 + g * 4:12 + g * 4, tk]
                P.tt(yo, pt[:, 0:512].rearrange("p (a b) -> p a b", b=128), bc(cl(C_NCG + g * 4, 4), [128]), ALU.mult)
                Sg = Sst[:, l, g, :]
                P.tt(x3(Sg), x3(Sg), bc(cd[:, g * 8:(g + 1) * 8], [64]), ALU.mult, eng="pool")
                yield
                P.tt(Sg, Sg, pst[:], ALU.add)
                yield
                P.copy(Sbf[:, g, :], Sg, eng="act")

            interleave([genG(0), genG(1)])
        if "o" in hooks:
            hooks["o"]()
        po = [ps() for _ in range(8)]
        for q in range(4):
            io = pidx[("O", q)]
            touch(io)
            WO = ring_view(io, [128, 4, 1024])
            for co in range(8):
                for kk in range(4):
                    P.mm(po[co][:], WO[:, kk, co * 128:(co + 1) * 128], yT[:, q * 4 + kk, :],
                         start=(q == 0 and kk == 0), stop=(q == 3 and kk == 3))
        for co in range(8):
            P.tt(xT[:, co, :], xT[:, co, :], po[co][:], ALU.add)
        if "ffn" in hooks:
            hooks["ffn"]()
        rs_ = rmsnorm_to_hT(l, C_G2)
        for kt in range(8):
            P.stt(hT[:, kt, :], xT[:, kt, :], cl(C_G2 + kt), rs_, ALU.mult, ALU.mult)
        for j in range(8):
            i1 = pidx[("F1", j)]
            touch(i1)
            W1 = ring_view(i1, [128, 8, 512])
            for c4 in range(4):
                pf = ps()
                for kt in range(8):
                    P.mm(pf[:], W1[:, kt, c4 * 128:(c4 + 1) * 128], hT[:, kt, :], start=(kt == 0), stop=(kt == 7))
                r_ = rl[c4 % 2]
                P.act(r_, pf[:], AF.Relu)
                ui = j * 4 + c4
                P.tt(uT[:, ui * 512:(ui + 1) * 512], r_, r_, ALU.mult, eng="pool")
        if "ff2" in hooks:
            hooks["ff2"]()
        po = [ps() for _ in range(8)]
        for q in range(8):
            i2 = pidx[("F2", q)]
            touch(i2)
            W2 = ring_view(i2, [128, 4, 1024])
            for co in range(8):
                for kk in range(4):
                    ui = q * 4 + kk
                    P.mm(po[co][:], W2[:, kk, co * 128:(co + 1) * 128], uT[:, ui * 512:(ui + 1) * 512],
                         start=(q == 0 and kk == 0), stop=(q == 7 and kk == 3))
        for co in range(8):
            P.tt(xT[:, co, :], xT[:, co, :], po[co][:], ALU.add)

    stage3 = stage.rearrange("p (c d) -> p c d", d=T)
    ostage3 = scr[:, 4096:8192].rearrange("p (c d) -> p c d", d=T)
    outs = []
    plans = [[[plan_layer(l) for l in range(DEPTH)] for ti in range(NT)] for b in range(NSEQ)]
    tiles = [(b, ti) for b in range(NSEQ) for ti in range(NT)]

    def xsrc(t_ap, b, ti):
        return bass.AP(t_ap.tensor, b * D * S + ti * T, [[S, 128], [128 * S, 8], [1, T]])

    for n, (b, ti) in enumerate(tiles):
        if n == 0:
            P.dma(stage3, xsrc(x_d, b, ti), eng="sp", grp="xin")
        for kt in range(8):
            P.copy(xT[:, kt, :], stage3[:, kt, :], eng=("act", "dve", "pool")[kt % 3])
        for l in range(DEPTH):
            hooks = {}
            if b == 0 and ti == 0 and l + 1 < DEPTH:
                hooks = {ph: (lambda ll=l + 1, bb=bi: prepass(ll, bb)) for bi, ph in enumerate(("a", "b", "c", "o"))}
            if l == DEPTH - 1 and n + 1 < len(tiles):
                nb, nti = tiles[n + 1]
                hooks["ffn"] = (lambda nb=nb, nti=nti: P.dma(stage3, xsrc(x_d, nb, nti), eng="sp", grp="xin"))
            block(l, ti == 0, plans[b][ti][l], hooks)
        rs_ = rmsnorm_to_hT(0, 0)
        for kt in range(8):
            P.stt(ostage3[:, kt, :], xT[:, kt, :], fin[:, kt:kt + 1], rs_, ALU.mult, ALU.mult)
        dst = xsrc(out_d, b, ti)
        P.dma(dst, ostage3, eng="sp", grp="xout")
        outs.append(dst)
    P.add("sp", lambda e: None, reads=outs)
    P.emit()
    return nc, P


def prep_weights(norm1_g, w_in, conv_a_w, conv_a_b, ln_a_g, ln_a_b, ln_b_g, ln_b_b, w_spatial, b_spatial,
                 conv_c_w, conv_c_b, dt_bias, a_log, d_skip, norm_c_g, w_out, norm2_g, w_ff1, w_ff2, final_g):
    L = w_in.shape[0]
    f = np.float32
    perm = []
    for ct in range(4):
        perm += list(range(ct * 128, (ct + 1) * 128)) + list(range(512 + ct * 128, 512 + (ct + 1) * 128))
    perm += list(range(1024, DIN))
    w_in_p = np.ascontiguousarray(np.asarray(w_in, f)[:, :, perm])
    cols = np.zeros((L, 128, NCOLS), f)
    colv = lambda v, n: np.asarray(v, f).reshape(L, n, 128).transpose(0, 2, 1)
    cols[:, :, C_G1:C_G1 + 8] = colv(norm1_g, 8)
    cols[:, :, C_G2:C_G2 + 8] = colv(norm2_g, 8)
    cols[:, :, C_CAB:C_CAB + 4] = colv(conv_a_b, 4)
    cols[:, :, C_LAG:C_LAG + 4] = colv(ln_a_g, 4)
    cols[:, :, C_LAB:C_LAB + 4] = colv(ln_a_b, 4)
    cols[:, :, C_CCB:C_CCB + 4] = colv(np.asarray(conv_c_b, f)[:, 1024:1536], 4)
    cols[:, :, C_BST:C_BST + 8] = np.asarray(b_spatial, f).transpose(0, 2, 1)
    cols[:, :, C_NCG:C_NCG + 8] = colv(norm_c_g, 8)
    caw = np.asarray(conv_a_w, f).reshape(L, 31, 4, 128).transpose(0, 3, 2, 1)
    cols[:, :, C_CAW:C_CAW + 124] = caw.reshape(L, 128, 124)
    ccw = np.asarray(conv_c_w, f).reshape(L, 4, 12, 128).transpose(0, 3, 2, 1)
    cols[:, :, C_CCW:C_CCW + 48] = ccw.reshape(L, 128, 48)
    rows = np.concatenate([np.asarray(ln_b_g, f), np.asarray(ln_b_b, f), np.asarray(conv_c_b, f),
                           np.asarray(dt_bias, f), np.asarray(a_log, f), np.asarray(d_skip, f)], axis=1)
    assert rows.shape[1] == NROWS
    wst = np.ascontiguousarray(np.asarray(w_spatial, f).transpose(0, 3, 1, 2)).reshape(L, 128, 1024)
    fin = np.ascontiguousarray(np.asarray(final_g, f).reshape(8, 128).T)
    return dict(w_in=w_in_p, w_out=np.ascontiguousarray(np.asarray(w_out, f)),
                w_ff1=np.ascontiguousarray(np.asarray(w_ff1, f)), w_ff2=np.ascontiguousarray(np.asarray(w_ff2, f)),
                cols=cols, rows=np.ascontiguousarray(rows), wst=wst, fin=fin)


_CACHE = {}


def run(x, weights, n_cores, trace=False):
    x = np.asarray(x, np.float32)
    B, S, _ = x.shape
    NSEQ = B // n_cores
    DEPTH = weights["w_in"].shape[0]
    key = (NSEQ, S, DEPTH)
    if key not in _CACHE:
        _CACHE[key] = build_nc(NSEQ, S, DEPTH)[0]
    nc = _CACHE[key]
    in_maps = []
    for c in range(n_cores):
        m = dict(weights)
        m["x"] = np.ascontiguousarray(x[c * NSEQ:(c + 1) * NSEQ].transpose(0, 2, 1))
        in_maps.append(m)
    res = run_bass_kernel_spmd(nc, in_maps, core_ids=list(range(n_cores)))
    return np.ascontiguousarray(np.concatenate([r["out"] for r in res.results], axis=0).transpose(0, 2, 1))


def kernel(x, **params):
    weights = prep_weights(**params)
    return run(x, weights, 8)
```

```python
import numpy as np
from contextlib import ExitStack
import concourse.bass as bass
import concourse.mybir as mybir
from concourse.bass_utils import run_bass_kernel_spmd

F32 = mybir.dt.float32
BF16 = mybir.dt.bfloat16
ALU = mybir.AluOpType
AF = mybir.ActivationFunctionType
AX = mybir.AxisListType
DSZ = {F32: 4, BF16: 2}
COMPUTE = ("pe", "act", "dve", "pool")
ALLENG = COMPUTE + ("sp",)
EPS = 1e-5


def _region(ap):
    name = ap.tensor.name
    sz = DSZ.get(ap.dtype, 4)
    dims = ap.ap
    off = ap.offset
    if "dram" in str(ap.space).lower() or "hbm" in str(ap.space).lower():
        ext = 1
        for s, c in dims:
            ext += (c - 1) * abs(s)
        return (name, 0, 1, off * sz, (off + ext) * sz)
    if name.startswith("ps"):
        return (name, 0, 128, 0, 2048)
    pstep = dims[0][0]
    npart = dims[0][1]
    if pstep == 0:
        pstep = 1 << 40
    p0 = off // pstep
    f0 = off % pstep
    ext = 1
    for s, c in dims[1:]:
        ext += (c - 1) * abs(s)
    return (name, p0, p0 + npart, f0 * sz, (f0 + ext) * sz)


def _ovl(a, b):
    return a[1] < b[2] and b[1] < a[2] and a[3] < b[4] and b[3] < a[4]


def _cov(a, b):
    return a[1] <= b[1] and a[2] >= b[2] and a[3] <= b[3] and a[4] >= b[4]


class Op:
    __slots__ = ("eng", "fn", "deps", "idx", "inc", "count", "grp", "waits")


class Prog:
    def __init__(self, nc):
        self.nc = nc
        self.ops = []
        self.W = {}
        self.R = {}
        self.groups = []

    def add(self, eng, fn, reads=(), writes=(), grp=None):
        op = Op()
        op.eng = eng
        op.fn = fn
        op.idx = len(self.ops)
        op.inc = False
        op.count = 0
        op.grp = grp
        op.waits = None
        if grp is not None and grp not in self.groups:
            self.groups.append(grp)
        deps = set()
        ordered = grp is None
        for ap in reads:
            r = _region(ap)
            wl = self.W.get(r[0])
            if wl:
                for e in wl:
                    if _ovl(e[0], r):
                        d = e[1]
                        if ordered and d.grp is None and d.eng == eng and eng == "pe":
                            continue
                        deps.add(d.idx)
            rl = self.R.setdefault(r[0], [])
            if ordered:
                rl[:] = [e for e in rl if not (e[1].grp is None and e[1].eng == eng and _cov(r, e[0]))]
            rl.append((r, op))
        for ap in writes:
            r = _region(ap)
            wl = self.W.setdefault(r[0], [])
            for e in wl:
                if _ovl(e[0], r):
                    d = e[1]
                    if ordered and d.grp is None and d.eng == eng and eng == "pe":
                        continue
                    deps.add(d.idx)
            rl = self.R.setdefault(r[0], [])
            for e in rl:
                if _ovl(e[0], r):
                    d = e[1]
                    if d is op:
                        continue
                    if ordered and d.grp is None and d.eng == eng and eng == "pe":
                        continue
                    deps.add(d.idx)
            wl[:] = [e for e in wl if not _cov(r, e[0])]
            rl[:] = [e for e in rl if not (_cov(r, e[0]) and e[1] is not op)]
            wl.append((r, op))
        best = {}
        keep = set()
        for d in deps:
            dop = self.ops[d]
            if dop.grp is None:
                if d > best.get(dop.eng, -1):
                    best[dop.eng] = d
            else:
                keep.add(d)
        keep.update(best.values())
        op.deps = keep
        self.ops.append(op)
        return op

    def mm(self, out, lhsT, rhs, start=True, stop=True):
        return self.add("pe", lambda e: e.matmul(out, lhsT, rhs, start=start, stop=stop),
                        reads=[lhsT, rhs], writes=[out])

    def transpose(self, out, in_, ident):
        return self.add("pe", lambda e: e.transpose(out, in_, ident), reads=[in_, ident], writes=[out])

    def act(self, out, in_, func, bias=0.0, scale=1.0, accum_out=None):
        reads = [in_]
        if not isinstance(bias, (int, float)):
            reads.append(bias)
        if not isinstance(scale, (int, float)):
            reads.append(scale)
        writes = [out]
        kw = {}
        if accum_out is not None:
            kw["accum_out"] = accum_out
            writes.append(accum_out)
        return self.add("act", lambda e: e.activation(out, in_, func, bias=bias, scale=scale, **kw),
                        reads=reads, writes=writes)

    def tt(self, out, in0, in1, op, eng="dve"):
        return self.add(eng, lambda e: e.tensor_tensor(out, in0, in1, op), reads=[in0, in1], writes=[out])

    def ts(self, out, in0, s1, s2, op0, op1=None, eng="dve"):
        reads = [in0]
        if not isinstance(s1, (int, float)):
            reads.append(s1)
        if s2 is not None and not isinstance(s2, (int, float)):
            reads.append(s2)
        kw = {}
        if op1 is not None:
            kw["op1"] = op1
        return self.add(eng, lambda e: e.tensor_scalar(out, in0, s1, s2, op0, **kw), reads=reads, writes=[out])

    def stt(self, out, in0, scalar, in1, op0, op1, eng="dve"):
        reads = [in0, in1]
        if not isinstance(scalar, (int, float)):
            reads.append(scalar)
        return self.add(eng, lambda e: e.scalar_tensor_tensor(out, in0, scalar, in1, op0, op1),
                        reads=reads, writes=[out])

    def copy(self, out, in_, eng="dve"):
        if eng == "act":
            return self.add(eng, lambda e: e.copy(out, in_), reads=[in_], writes=[out])
        return self.add(eng, lambda e: e.tensor_copy(out, in_), reads=[in_], writes=[out])

    def recip(self, out, in_):
        return self.add("dve", lambda e: e.reciprocal(out, in_), reads=[in_], writes=[out])

    def reduce(self, out, in_, eng="dve"):
        return self.add(eng, lambda e: e.tensor_reduce(out, in_, AX.X, ALU.add), reads=[in_], writes=[out])

    def memset(self, ap, val, eng="pool"):
        return self.add(eng, lambda e: e.memset(ap, val), reads=[], writes=[ap])

    def asel(self, ap, pattern, cmp, fill, base=0, cm=0):
        return self.add("pool", lambda e: e.affine_select(ap, ap, pattern, cmp, fill, base=base,
                                                          channel_multiplier=cm), reads=[ap], writes=[ap])

    def dma(self, out, in_, eng="sp", grp="g0"):
        return self.add(eng, lambda e: e.dma_start(out, in_), reads=[in_], writes=[out], grp=grp)

    def finalize(self):
        ops = self.ops
        for op in ops:
            for d in op.deps:
                if ops[d].grp is None:
                    ops[d].inc = True
        cnt = {e: 0 for e in ALLENG}
        tot = {g: 0 for g in self.groups}
        seen = {e: {} for e in ALLENG}
        know = {e: {} for e in ALLENG}
        nw = 0
        for op in ops:
            w = {}
            for d in op.deps:
                dop = ops[d]
                if dop.grp is None:
                    key = ("e", dop.eng)
                    val = dop.count
                else:
                    key = ("g", dop.grp)
                    val = tot[dop.grp]
                if val > w.get(key, 0):
                    w[key] = val
            s = seen[op.eng]
            op.waits = []
            for key, val in sorted(w.items(), key=lambda kv: -kv[1]):
                if val > s.get(key, 0):
                    s[key] = val
                    op.waits.append((key, val))
                    nw += 1
                    if key[0] == "e":
                        kd = know[key[1]].get(val)
                        if kd:
                            for k2, v2 in kd.items():
                                if v2 > s.get(k2, 0):
                                    s[k2] = v2
            if op.grp is None:
                if op.inc:
                    cnt[op.eng] += 1
                    kd = dict(s)
                    kd[("e", op.eng)] = cnt[op.eng]
                    know[op.eng][cnt[op.eng]] = kd
                op.count = cnt[op.eng]
            else:
                tot[op.grp] += 16
        self.stats = dict(n_ops=len(ops), n_waits=nw, cnt=cnt, ngroups=len(self.groups))

    def emit(self):
        nc = self.nc
        self.finalize()
        with ExitStack() as es:
            sems = {}
            for e in ALLENG:
                sems[("e", e)] = es.enter_context(nc.semaphore("s_" + e))
            for g in self.groups:
                sems[("g", g)] = es.enter_context(nc.semaphore("g_" + str(g)))
            block = es.enter_context(nc.Block())
            by_eng = {e: [] for e in ALLENG}
            for op in self.ops:
                by_eng[op.eng].append(op)

            def run(engobj, name):
                for op in by_eng[name]:
                    for key, val in op.waits:
                        engobj.wait_ge(sems[key], val)
                    ins = op.fn(engobj)
                    if ins is None:
                        continue
                    if op.grp is not None:
                        ins.then_inc(sems[("g", op.grp)], 16)
                    elif op.inc:
                        ins.then_inc(sems[("e", name)], 1)

            @block.sync
            def _(e):
                run(e, "sp")

            @block.tensor
            def _(e):
                run(e, "pe")

            @block.scalar
            def _(e):
                run(e, "act")

            @block.vector
            def _(e):
                run(e, "dve")

            @block.gpsimd
            def _(e):
                run(e, "pool")


def bc(ap, counts):
    dims = [list(d) for d in ap.ap]
    for c in counts:
        dims.append([0, c])
    return bass.AP(ap.tensor, ap.offset, dims)


D = 1024
DIN = 4624
NCOLS = 220
C_G1, C_G2, C_CAB, C_LAG, C_LAB, C_CCB, C_BST, C_NCG, C_CAW, C_CCW = 0, 8, 16, 20, 24, 28, 32, 40, 48, 172
NROWS = 512 + 512 + 1536 + 48
R_LBG, R_LBB, R_CCB, R_DTB, R_ALOG, R_DSK = 0, 512, 1024, 2560, 2576, 2592
T = 512
NCH = 4
NSLOT = 6
LOOKAHEAD = 4


def build_nc(NSEQ, S, DEPTH):
    nc = bass.Bass("TRN2", target_bir_lowering=False)
    P = Prog(nc)
    NT = S // T

    def din(name, shape, dt=F32):
        return nc.dram_tensor(name, shape, dt, kind="ExternalInput").ap()

    x_d = din("x", [NSEQ, D, S])
    w_in_d = din("w_in", [DEPTH, D, DIN])
    w_out_d = din("w_out", [DEPTH, 2048, D])
    w_ff1_d = din("w_ff1", [DEPTH, D, 4096])
    w_ff2_d = din("w_ff2", [DEPTH, 4096, D])
    cols_d = din("cols", [DEPTH, 128, NCOLS])
    rows_d = din("rows", [DEPTH, NROWS])
    wst_d = din("wst", [DEPTH, 128, 1024])
    fin_d = din("fin", [128, 8])
    out_d = nc.dram_tensor("out", [NSEQ, D, S], F32, kind="ExternalOutput").ap()
    diagA_d = nc.dram_tensor("diagA", [DEPTH, 4, 128, 31 * 128], BF16).ap()
    diagC_d = nc.dram_tensor("diagC", [DEPTH, 128, 48 * 128], BF16).ap()
    NPAN = 31
    wbf_d = nc.dram_tensor("wbf", [DEPTH, NPAN, 128, 4096], BF16).ap()
    wdtb_d = nc.dram_tensor("wdtb", [DEPTH, 128, 128], BF16).ap()

    def wsrc(t_ap, base, rowlen, nk, ncol, col0):
        return bass.AP(t_ap.tensor, base + col0, [[rowlen, 128], [128 * rowlen, nk], [1, ncol]])

    def panel_specs(l):
        b_in = l * D * DIN
        sp_ = []
        for ct in range(4):
            sp_.append((("A", ct), [128, 8, 256], wsrc(w_in_d, b_in, DIN, 8, 256, ct * 256)))
        for nm, c0 in (("U", 1024), ("V", 1536), ("X0", 3072), ("X1", 3584), ("X2", 4096), ("Z0", 2048), ("Z1", 2560)):
            sp_.append((nm, [128, 8, 512], wsrc(w_in_d, b_in, DIN, 8, 512, c0)))
        for q in range(4):
            sp_.append((("O", q), [128, 4, 1024], wsrc(w_out_d, l * 2048 * D + q * 4 * 128 * D, D, 4, 1024, 0)))
        for j in range(8):
            sp_.append((("F1", j), [128, 8, 512], wsrc(w_ff1_d, l * D * 4096, 4096, 8, 512, j * 512)))
        for q in range(8):
            sp_.append((("F2", q), [128, 4, 1024], wsrc(w_ff2_d, l * 4096 * D + q * 4 * 128 * D, D, 4, 1024, 0)))
        assert len(sp_) == NPAN
        return sp_

    ppdummy = nc.dram_tensor("ppdummy", [8, 16], F32).ap()
    ppi = [0]

    def prepass(l, batch=None):
        def one(dst, src):
            gi = ppi[0] % 8
            ppi[0] += 1
            P.add("pool", lambda e: e.dma_start(dst, src), reads=[src], writes=[dst, ppdummy[gi:gi + 1, :]],
                  grp="pp%d_%d" % (l, gi))
        jobs = []
        for pi, (nm, shape, src) in enumerate(panel_specs(l)):
            n = shape[1] * shape[2]
            jobs.append((wbf_d[l, pi][:, 0:n].rearrange("p (a b) -> p a b", b=shape[2]), src))
        jobs.append((wdtb_d[l].rearrange("p (a b) -> p a b", b=16), wsrc(w_in_d, l * D * DIN, DIN, 8, 16, 4608)))
        for i, (dst, src) in enumerate(jobs):
            if batch is None or i // 8 == batch:
                one(dst, src)

    def sb(name, shape, dt=F32):
        return nc.alloc_sbuf_tensor(name, shape, dt)

    xT = sb("xT", [128, 8, T])
    hT = sb("hT", [128, 8, T], BF16)
    ybuf = sb("ybuf", [128, 16 * T], BF16)
    yT = ybuf[:].rearrange("p (a b) -> p a b", b=T)
    ring = sb("ring", [128, NSLOT, 4096], BF16)
    wdt = sb("wdt", [128, 8, 16], BF16)
    dC = sb("dC", [128, 48 * 128], BF16)
    Sst = sb("Sst", [128, DEPTH, 2, 512])
    Sbf = sb("Sbf", [128, 2, 512], BF16)
    histA = sb("histA", [128, DEPTH, 4, 30], BF16)
    histC = sb("histC", [128, DEPTH, 12, 4], BF16)
    ident = sb("ident", [128, 128])
    identb = sb("identb", [128, 128], BF16)
    ones = sb("ones", [128, 128])
    Bd64 = sb("Bd64", [128, 128])
    Utri = sb("Utri", [128, 128])
    maskneg = sb("maskneg", [128, 128])
    Esel = sb("Esel", [48, 16, 128], BF16)
    maskb = sb("maskb", [128, 128], BF16)
    Bd64b = sb("Bd64b", [128, 128], BF16)
    Utrib = sb("Utrib", [128, 128], BF16)
    onesbb = sb("onesbb", [128, 128], BF16)
    onesb = sb("onesb", [2, 128], BF16)
    acs2 = sb("acs2", [48, 128], BF16)
    wsT = sb("wsT", [128, 1024], BF16)
    cols = sb("cols_sb", [128, DEPTH, NCOLS])
    fin = sb("fin_sb", [128, 8])
    hb16 = sb("hb16", [128, DEPTH, 48])
    lnb = sb("lnb", [128, 1024])
    ccb2 = sb("ccb2", [2, 1536], BF16)
    ccb_d = nc.dram_tensor("ccbd", [DEPTH, 2, 1536], BF16).ap()
    SCRW = 13056
    scr = sb("scr", [128, SCRW])
    wsm_d = nc.dram_tensor("wsm", [DEPTH, 128, 1024], BF16).ap()
    ps_banks = [nc.alloc_psum_tensor("ps%d" % i, [128, 512], F32) for i in range(8)]
    psi = [0]

    def ps():
        b = ps_banks[psi[0] % 8]
        psi[0] += 1
        return b

    soff = [0]

    def carve(nwords, dt=F32, shape=None):
        a = scr[:, soff[0]:soff[0] + nwords]
        soff[0] += nwords
        assert soff[0] <= SCRW, soff[0]
        if dt == BF16:
            a = a.bitcast(BF16)
        return a

    sqA = carve(512); sqB = carve(512); rs = carve(512)
    st8 = carve(64)
    base = soff[0]
    SA = []
    for _ in range(2):
        SA.append(dict(sg=carve(512), hb=carve(272, BF16), cA=carve(512), cA2=carve(512), cAh=carve(256, BF16), msq=carve(512),
                       var=carve(512), t1=carve(512)))
    soff[0] = base
    xbuf = carve(12 * 258, BF16)
    BCcm = carve(4 * 256, BF16)
    xs = carve(1024)
    xc = carve(512, BF16); xcd = carve(512, BF16)
    Btok = carve(128, BF16)
    sm = carve(256)
    CBT = [carve(128), carve(128)]
    LT = [[carve(128), carve(128)], [carve(128), carve(128)]]
    MT = [[carve(64, BF16), carve(64, BF16)], [carve(64, BF16), carve(64, BF16)]]
    y1 = [carve(512), carve(512)]; xd = [carve(512), carve(512)]; sz = [carve(512), carve(512)]
    yct = [carve(256, BF16), carve(256, BF16)]
    c_end = soff[0]
    soff[0] = c_end - 2 * 3136
    assert soff[0] >= base + 12 * 258 + 4 * 256
    SB = []
    for _ in range(2):
        SB.append(dict(gu=carve(512), gv=carve(512), gv2=carve(512), vn=carve(512), vnb=carve(256, BF16),
                       ybt=carve(256, BF16), tmpB=carve(512), st=carve(64)))
    assert soff[0] == c_end
    uT = scr[:, 0:8192].bitcast(BF16)
    rl = [scr[:, 8192:8704], scr[:, 8704:9216]]
    stage = ybuf[:].bitcast(F32)

    xbv = lambda ct, a, b: xbuf[:, ct * 516 + a: ct * 516 + b]
    BCv = lambda i, a, b: BCcm[:, i * 512 + a: i * 512 + b]

    P.memset(ident[:], 0.0)
    P.asel(ident[:], [[-1, 128]], ALU.not_equal, 1.0, base=0, cm=1)
    P.copy(identb[:], ident[:], eng="dve")
    P.memset(ones[:], 1.0)
    P.memset(Bd64[:], 0.0)
    P.memset(Bd64[0:64, 0:64], 1.0 / 64)
    P.memset(Bd64[64:128, 64:128], 1.0 / 64)
    P.memset(Utri[:], 1.0)
    P.asel(Utri[:], [[1, 128]], ALU.is_ge, 0.0, base=0, cm=-1)
    P.memset(maskneg[:], 0.0)
    P.asel(maskneg[:], [[1, 128]], ALU.is_ge, -30000.0, base=0, cm=-1)
    P.memset(Esel[:], 0.0)
    P.asel(Esel[0:16], [[-1, 16], [0, 128]], ALU.not_equal, 1.0, base=0, cm=1)
    P.dma(Esel[32:48], Esel[0:16], eng="sp", grp="c3")
    P.copy(maskb[:], maskneg[:], eng="dve")
    P.copy(Bd64b[:], Bd64[:], eng="dve")
    P.copy(Utrib[:], Utri[:], eng="dve")
    P.memset(onesbb[:], 1.0)
    P.memset(onesb[:], 1.0)
    P.memset(acs2[:], 0.0)
    P.dma(fin[:], fin_d, eng="sp", grp="c0")
    for l in range(DEPTH):
        P.dma(wsT[:], wst_d[l], eng="pool", grp="c1")
        P.asel(wsT[:].rearrange("p (h t) -> p h t", t=128), [[0, 8], [1, 128]], ALU.is_ge, 0.0, base=0, cm=-1)
        P.dma(wsm_d[l], wsT[:], eng="sp", grp="c2")
    prepass(0)
    dtmp = [scr[:, 0:1984].bitcast(BF16), scr[:, 1984:3968].bitcast(BF16)]
    dtmpC = scr[:, 3968:3968 + 3072].bitcast(BF16)
    for l in range(DEPTH):
        P.dma(cols[:, l, :], cols_d[l], eng="sp", grp="c0")
        P.dma(hb16[:, l, :], bass.AP(rows_d.tensor, l * NROWS + R_DTB, [[0, 128], [1, 48]]), eng="sp", grp="c0")
        btmp = scr[0:1, 8000:8000 + 1536]
        bhi = scr[0:1, 9600:9600 + 768].bitcast(BF16)
        blo = scr[0:1, 10400:10400 + 768].bitcast(BF16)
        P.dma(btmp, rows_d[l:l + 1, R_CCB:R_CCB + 1536], eng="sp", grp="c0")
        P.copy(bhi, btmp, eng="act")
        P.tt(blo, btmp, bhi, ALU.subtract)
        P.dma(ccb_d[l, 0:1, :], bhi, eng="sp", grp="c2")
        P.dma(ccb_d[l, 1:2, :], blo, eng="sp", grp="c2")
        P.act(hb16[:, l, 16:32], hb16[:, l, 16:32], AF.Exp)
        P.ts(hb16[:, l, 16:32], hb16[:, l, 16:32], -1.0, None, ALU.mult)
        for ct in range(4):
            dt_ = dtmp[ct % 2]
            for k in range(31):
                c = C_CAW + ct * 31 + k
                if k % 2 == 0:
                    P.act(dt_[:, k * 128:(k + 1) * 128], identb[:], AF.Copy, scale=cols[:, l, c:c + 1])
                else:
                    P.ts(dt_[:, k * 128:(k + 1) * 128], identb[:], cols[:, l, c:c + 1], None, ALU.mult)
            P.dma(diagA_d[l, ct], dt_[:, 0:3968], eng="sp", grp="c2")
        for ct in range(12):
            for k in range(4):
                c = C_CCW + ct * 4 + k
                i = ct * 4 + k
                if i % 2 == 0:
                    P.act(dtmpC[:, i * 128:(i + 1) * 128], identb[:], AF.Copy, scale=cols[:, l, c:c + 1])
                else:
                    P.ts(dtmpC[:, i * 128:(i + 1) * 128], identb[:], cols[:, l, c:c + 1], None, ALU.mult)
        P.dma(diagC_d[l], dtmpC[:, 0:6144], eng="sp", grp="c2")

    reqs = []
    issued = [0]

    def ring_view(i, shape):
        slot = i % NSLOT
        a = ring[:, slot, :]
        if len(shape) == 2:
            return a[:, 0:shape[1]]
        return a[:, 0:shape[1] * shape[2]].rearrange("p (a b) -> p a b", b=shape[2])

    def touch(i, keep=2):
        while issued[0] <= min(i + NSLOT - keep, len(reqs) - 1):
            j = issued[0]
            n, src, eng = reqs[j]
            P.dma(ring[:, j % NSLOT, 0:n], src, eng=eng, grp=("w%d%s" % (j % NSLOT, eng)))
            issued[0] += 1

    def plan_layer(l):
        idx = {}
        specs = {nm: (pi, shape) for pi, (nm, shape, src) in enumerate(panel_specs(l))}

        def req(nm):
            pi, shape = specs[nm]
            n = shape[1] * shape[2]
            idx[nm] = len(reqs)
            reqs.append((n, wbf_d[l, pi][:, 0:n], "sp"))

        def reqd(ct):
            idx[("dA", ct)] = len(reqs)
            reqs.append((3968, diagA_d[l, ct], "sp"))

        for pair in range(2):
            req(("A", 2 * pair)); req(("A", 2 * pair + 1)); reqd(2 * pair); reqd(2 * pair + 1)
        for nm in ("U", "V", "X0", "X1", "X2", "Z0", "Z1"):
            req(nm)
        for q in range(4):
            req(("O", q))
        for j in range(8):
            req(("F1", j))
        for q in range(8):
            req(("F2", q))
        return idx

    def interleave(gens):
        gens = list(gens)
        while gens:
            nxt = []
            for g_ in gens:
                try:
                    next(g_)
                    nxt.append(g_)
                except StopIteration:
                    pass
            gens = nxt

    def rmsnorm_to_hT(l, gcol, gsrc=None):
        pss = ps()
        for kt in range(8):
            sq = (sqA if kt % 2 == 0 else sqB).bitcast(BF16)[:, 0:512]
            if kt % 2 == 0:
                P.act(sq, xT[:, kt, :], AF.Square)
            else:
                P.tt(sq, xT[:, kt, :], xT[:, kt, :], ALU.mult, eng="pool")
            P.mm(pss[:], onesbb[:], sq, start=(kt == 0), stop=(kt == 7))
        P.act(rs, pss[:], AF.Ln, bias=EPS, scale=1.0 / D)
        P.act(rs, rs, AF.Exp, scale=-0.5)
        return rs

    def block(l, first, pidx, hooks=None):
        hooks = hooks or {}
        cl = lambda c0, n=1: cols[:, l, c0:c0 + n]
        P.dma(lnb[:], bass.AP(rows_d.tensor, l * NROWS + R_LBG, [[0, 128], [1, 1024]]), eng="sp", grp="r0")
        P.dma(ccb2[:], ccb_d[l], eng="sp", grp="r1")
        P.dma(dC[:], diagC_d[l], eng="sp", grp="r2")
        P.dma(wsT[:], wsm_d[l], eng="sp", grp="r4")
        P.dma(wdt[:], wdtb_d[l].rearrange("p (a b) -> p a b", b=16), eng="sp", grp="r3")
        rs_ = rmsnorm_to_hT(l, C_G1)
        for kt in range(8):
            P.stt(hT[:, kt, :], xT[:, kt, :], cl(C_G1 + kt), rs_, ALU.mult, ALU.mult)
        if "a" in hooks:
            hooks["a"]()
        def genA(ct):
            S_ = SA[ct % 2]
            sg, hb, cA, cA2, msq, var, t1 = (S_[k] for k in ("sg", "hb", "cA", "cA2", "msq", "var", "t1"))
            ia = pidx[("A", ct)]
            touch(ia, keep=3)
            WA = ring_view(ia, [128, 8, 256])
            pv = ps(); pg = ps()
            for kt in range(8):
                P.mm(pv[:], WA[:, kt, 0:128], hT[:, kt, :], start=(kt == 0), stop=(kt == 7))
            for kt in range(8):
                P.mm(pg[:], WA[:, kt, 128:256], hT[:, kt, :], start=(kt == 0), stop=(kt == 7))
            yield
            P.act(sg, pg[:], AF.Sigmoid)
            if first:
                P.memset(hb[:, 0:30], 0.0)
            else:
                P.copy(hb[:, 0:30], histA[:, l, ct, :], eng="pool")
            P.tt(hb[:, 30:542], pv[:], sg, ALU.mult)
            P.copy(histA[:, l, ct, :], hb[:, 512:542], eng="pool")
            ida = pidx[("dA", ct)]
            touch(ida, keep=3)
            dA = ring_view(ida, [128, 3968])
            pc = ps()
            for k in range(31):
                P.mm(pc[:], dA[:, k * 128:(k + 1) * 128], hb[:, k:k + 512], start=(k == 0), stop=(k == 30))
            yield
            cAh = S_["cAh"]
            cA2b = cA2.bitcast(BF16)[:, 0:512]
            P.act(cA, pc[:], AF.Identity, bias=cl(C_CAB + ct))
            P.act(cA2b, pc[:], AF.Square, bias=cl(C_CAB + ct))
            P.copy(cAh, cA, eng="pool")
            pm = ps(); pe = ps()
            P.mm(pm[:], Bd64b[:], cAh)
            P.mm(pe[:], Bd64b[:], cA2b)
            yield
            P.act(msq, pm[:], AF.Square)
            P.tt(var, pe[:], msq, ALU.subtract)
            P.tt(t1, cA, pm[:], ALU.subtract)
            yield
            P.act(var, var, AF.Ln, bias=EPS)
            P.act(var, var, AF.Exp, scale=-0.5)
            yield
            P.tt(t1, t1, var, ALU.mult, eng="pool")
            yield
            P.act(yT[:, ct, :], t1, AF.Silu, bias=cl(C_LAB + ct), scale=cl(C_LAG + ct))

        for pair in range(2):
            interleave([genA(2 * pair), genA(2 * pair + 1)])

        if "b" in hooks:
            hooks["b"]()
        iu = pidx["U"]; iv = pidx["V"]
        touch(iu); touch(iv)
        WU = ring_view(iu, [128, 8, 512]); WV = ring_view(iv, [128, 8, 512])
        v3 = lambda a_: a_.rearrange("p (h d) -> p h d", d=64)

        def genB(j):
            S_ = SB[j % 2]
            gu, gv, gv2, vn, vnb, ybt, tmpB, st = (S_[k] for k in ("gu", "gv", "gv2", "vn", "vnb", "ybt", "tmpB", "st"))
            tk = slice(j * 128, (j + 1) * 128)
            pu = ps(); pvv = ps()
            for kt in range(8):
                P.mm(pu[:], hT[:, kt, tk], WU[:, kt, :], start=(kt == 0), stop=(kt == 7))
            for kt in range(8):
                P.mm(pvv[:], hT[:, kt, tk], WV[:, kt, :], start=(kt == 0), stop=(kt == 7))
            yield
            P.act(gv, pvv[:], AF.Gelu)
            P.act(gu, pu[:], AF.Gelu)
            yield
            s1 = st[:, 0:8]; s2 = st[:, 8:16]; mean = st[:, 16:24]; mq = st[:, 24:32]; vr = st[:, 32:40]
            P.reduce(s1, v3(gv))
            P.tt(gv2, gv, gv, ALU.mult, eng="pool")
            yield
            P.reduce(s2, v3(gv2))
            P.ts(mean, s1, 1.0 / 64, None, ALU.mult)
            P.tt(mq, mean, mean, ALU.mult)
            P.stt(vr, s2, 1.0 / 64, mq, ALU.mult, ALU.subtract)
            yield
            P.act(vr, vr, AF.Ln, bias=EPS)
            P.act(vr, vr, AF.Exp, scale=-0.5)
            P.tt(v3(vn), v3(gv), bc(mean, [64]), ALU.subtract)
            yield
            P.tt(v3(vn), v3(vn), bc(vr, [64]), ALU.mult)
            yield
            P.tt(vn, vn, lnb[:, 0:512], ALU.mult)
            P.tt(vnb, vn, lnb[:, 512:1024], ALU.add, eng="pool")
            yield
            pmx = ps()
            for h in range(8):
                P.mm(pmx[:, h * 64:(h + 1) * 64], wsT[:, h * 128:(h + 1) * 128], vnb[:, h * 64:(h + 1) * 64])
            yield
            P.tt(v3(tmpB), v3(pmx[:]), bc(cl(C_BST, 8), [64]), ALU.add)
            yield
            P.tt(ybt, tmpB, gu, ALU.mult, eng="pool")
            yield
            pt = ps()[:].bitcast(BF16)
            for c4 in range(4):
                P.transpose(pt[:, c4 * 128:(c4 + 1) * 128], ybt[:, c4 * 128:(c4 + 1) * 128], identb[:])
            yield
            P.add("act", lambda e, o=yT[:, 4:8, tk], i=pt[:, 0:512].rearrange("p (a b) -> p a b", b=128): e.copy(o, i),
                  reads=[pt[:, 0:512]], writes=[yT[:, 4:8, tk]])

        def genX(cts, with_cm):
            for ct in cts:
                ix = pidx["X%d" % (ct // 4)]
                touch(ix, keep=3 + ct // 4)
                WX = ring_view(ix, [128, 8, 512])
                px = ps()
                cc = (ct % 4) * 128
                for kt in range(8):
                    P.mm(px[:], WX[:, kt, cc:cc + 128], hT[:, kt, :], start=(kt == 0), stop=(kt == 7))
                if first:
                    P.memset(xbv(ct, 0, 3), 0.0)
                else:
                    P.copy(xbv(ct, 0, 3), histC[:, l, ct, 0:3], eng="pool")
                yield
                P.copy(xbv(ct, 3, 515), px[:], eng="act")
                P.copy(histC[:, l, ct, 0:3], xbv(ct, 512, 515), eng="pool")
                yield
            if with_cm:
                for i in range(4):
                    ct = 8 + i
                    pc = ps()
                    for k in range(4):
                        P.mm(pc[:], dC[:, (ct * 4 + k) * 128:(ct * 4 + k + 1) * 128], xbv(ct, k, k + 512),
                             start=(k == 0), stop=(k == 3))
                    yield
                    P.act(BCv(i, 0, 512), pc[:], AF.Silu, bias=cl(C_CCB + i))
                    yield

        interleave([genB(0), genB(1), genX(range(0, 6), False)])
        interleave([genB(2), genB(3), genX(range(6, 12), True)])

        if "c" in hooks:
            hooks["c"]()
        if first:
            P.memset(Sst[:, l, :, :], 0.0)
        for g in range(2):
            P.copy(Sbf[:, g, :], Sst[:, l, g, :], eng="pool")
        iz = [pidx["Z0"], pidx["Z1"]]
        hbl = lambda a_, n=16: hb16[:, l, a_:a_ + n]
        x3 = lambda a_: a_.rearrange("p (h d) -> p h d", d=64)
        for j in range(NCH):
            tk = slice(j * 128, (j + 1) * 128)
            c0 = j * 128
            for half in range(2):
                pxs = ps()
                for i in range(4):
                    ct = half * 4 + i
                    o = pxs[:, i * 128:(i + 1) * 128]
                    for k in range(4):
                        P.mm(o, xbv(ct, c0 + k, c0 + k + 128), dC[:, (ct * 4 + k) * 128:(ct * 4 + k + 1) * 128],
                             start=(k == 0), stop=False)
                    P.mm(o, onesb[0:2, 0:128], ccb2[0:2, ct * 128:(ct + 1) * 128], start=False, stop=True)
                P.act(xs[:, half * 512:(half + 1) * 512], pxs[:], AF.Silu)
            pb = ps()
            for i in range(2):
                ct = 8 + i
                o = pb[:, i * 128:(i + 1) * 128]
                for k in range(4):
                    P.mm(o, xbv(ct, c0 + k, c0 + k + 128), dC[:, (ct * 4 + k) * 128:(ct * 4 + k + 1) * 128],
                         start=(k == 0), stop=False)
                P.mm(o, onesb[0:2, 0:128], ccb2[0:2, ct * 128:(ct + 1) * 128], start=False, stop=True)
            P.act(Btok, pb[:, 0:256], AF.Silu)
            pd = ps()
            for kt in range(8):
                P.mm(pd[:, 0:16], hT[:, kt, tk], wdt[:, kt, :], start=(kt == 0), stop=(kt == 7))
            dtv = sm[:, 0:16]; dt = sm[:, 16:32]; acs = sm[:, 48:64]; nacs = sm[:, 64:80]
            ea = sm[:, 80:96]; dec = sm[:, 96:112]; cd = sm[:, 112:128]; dtd = sm[:, 128:144]
            avp = sm[:, 160:208]
            av = avp[:, 0:16]
            P.tt(dtv, pd[:, 0:16], hbl(0), ALU.add)
            P.act(dtv, dtv, AF.Exp)
            P.act(dt, dtv, AF.Ln, bias=1.0)
            P.memset(avp[:, 16:32], 0.0)
            P.tt(av, dt, hbl(16), ALU.mult)
            P.tt(avp[:, 32:48], dt, hbl(16), ALU.mult)
            ab = sm[:, 208:256].bitcast(BF16)
            ahi = ab[:, 0:48]; alo = ab[:, 48:96]
            P.copy(ahi, avp, eng="pool")
            P.tt(alo, avp, ahi, ALU.subtract)
            pa = ps()
            P.mm(pa[:, 0:16], Utrib[:], ahi[:, 0:16], start=True, stop=False)
            P.mm(pa[:, 0:16], Utrib[:], alo[:, 0:16], start=False, stop=True)
            P.mm(pa[:, 16:32], onesbb[:], ahi[:, 0:16], start=True, stop=False)
            P.mm(pa[:, 16:32], onesbb[:], alo[:, 0:16], start=False, stop=True)
            P.mm(pa[0:48, 32:160], ahi, Utrib[:], start=True, stop=False)
            P.mm(pa[0:48, 32:160], alo, Utrib[:], start=False, stop=True)
            P.copy(acs, pa[:, 0:16], eng="dve")
            P.ts(nacs, pa[:, 0:16], -1.0, None, ALU.mult)
            P.act(ea, pa[:, 0:16], AF.Exp)
            P.tt(dec, pa[:, 16:32], acs, ALU.subtract)
            P.act(dec, dec, AF.Exp)
            P.act(cd, pa[:, 16:32], AF.Exp)
            P.copy(acs2[0:48, :], pa[0:48, 32:160], eng="act")
            P.tt(acs2[32:48, :], pa[32:48, 32:160], acs2[32:48, :], ALU.subtract)
            P.tt(dtd, dt, dec, ALU.mult)
            P.tt(x3(xc), x3(xs), bc(dt, [64]), ALU.mult)
            P.tt(x3(xcd), x3(xs), bc(dtd, [64]), ALU.mult, eng="pool")
            for g in range(2):
                touch(iz[g])
                WZ = ring_view(iz[g], [128, 8, 512])
                pz = ps()
                for kt in range(8):
                    P.mm(pz[:], hT[:, kt, tk], WZ[:, kt, :], start=(kt == 0), stop=(kt == 7))
                P.act(sz[g], pz[:], AF.Silu)
            def genG(g):
                y1g = y1[g]; xdg = xd[g]; yctg = yct[g]; CBTg = CBT[g]
                gs = slice(g * 512, (g + 1) * 512)
                bk = ps_banks[4 * g:4 * g + 4]
                pcb = bk[0]
                P.mm(pcb[:, 0:128], BCv(g, c0, c0 + 128), BCv(2 + g, c0, c0 + 128))
                pyo = bk[1]
                P.mm(pyo[:], BCv(2 + g, c0, c0 + 128), Sbf[:, g, :])
                yield
                P.copy(CBTg, pcb[:, 0:128], eng="act")
                P.tt(x3(xdg), x3(xs[:, gs]), bc(hbl(32 + g * 8, 8), [64]), ALU.mult, eng="pool")
                P.tt(x3(y1g), x3(pyo[:]), bc(ea[:, g * 8:(g + 1) * 8], [64]), ALU.mult)
                P.tt(y1g, y1g, xdg, ALU.add, eng="pool")
                pyd = bk[2]
                pLs = [bk[3], bk[0]]
                for quad in range(2):
                    pL = pLs[quad]
                    for q4 in range(4):
                        h = g * 8 + quad * 4 + q4
                        o = pL[:, q4 * 128:(q4 + 1) * 128]
                        P.mm(o, Esel[:, h, :], acs2[0:48, :], start=True, stop=False)
                        P.mm(o, identb[:], maskb[:], start=False, stop=True)
                    yield
                for quad in range(2):
                    pL = pLs[quad]
                    for q4 in range(4):
                        hh = quad * 4 + q4
                        h = g * 8 + hh
                        o = pL[:, q4 * 128:(q4 + 1) * 128]
                        lt = LT[g][hh % 2]; mt = MT[g][hh % 2]
                        P.act(lt, o, AF.Exp, bias=nacs[:, h:h + 1])
                        P.tt(mt, lt, CBTg, ALU.mult, eng=("dve" if hh % 2 == 0 else "pool"))
                        P.mm(pyd[:, hh * 64:(hh + 1) * 64], mt, xc[:, h * 64:(h + 1) * 64])
                    yield
                P.tt(y1g, y1g, pyd[:], ALU.add)
                yield
                P.tt(y1g, y1g, sz[g], ALU.mult)
                yield
                ssq = st8[:, 40 + 2 * g:41 + 2 * g]; sd = st8[:, 41 + 2 * g:42 + 2 * g]
                P.act(xdg, y1g, AF.Square, accum_out=ssq)
                P.act(sd, ssq, AF.Ln, bias=EPS, scale=1.0 / 512)
                P.act(sd, sd, AF.Exp, scale=-0.5)
                yield
                P.ts(yctg, y1g, sd, None, ALU.mult)
                yield
                pt = bk[3][:].bitcast(BF16)
                for c4 in range(4):
                    P.transpose(pt[:, c4 * 128:(c4 + 1) * 128], yctg[:, c4 * 128:(c4 + 1) * 128], identb[:])
                pst = bk[0]
                P.mm(pst[:], Btok[:, g * 128:(g + 1) * 128], xcd[:, gs])
                yield
                yo = yT[:, 8 + g * 4:12 + g * 4, tk]
                P.tt(yo, pt[:, 0:512].rearrange("p (a b) -> p a b", b=128), bc(cl(C_NCG + g * 4, 4), [128]), ALU.mult)
                Sg = Sst[:, l, g, :]
                P.tt(x3(Sg), x3(Sg), bc(cd[:, g * 8:(g + 1) * 8], [64]), ALU.mult, eng="pool")
                yield
                P.tt(Sg, Sg, pst[:], ALU.add)
                yield
                P.copy(Sbf[:, g, :], Sg, eng="act")

            interleave([genG(0), genG(1)])
        if "o" in hooks:
            hooks["o"]()
        po = [ps() for _ in range(8)]
        for q in range(4):
            io = pidx[("O", q)]
            touch(io)
            WO = ring_view(io, [128, 4, 1024])
            for co in range(8):
                for kk in range(4):
                    P.mm(po[co][:], WO[:, kk, co * 128:(co + 1) * 128], yT[:, q * 4 + kk, :],
                         start=(q == 0 and kk == 0), stop=(q == 3 and kk == 3))
        for co in range(8):
            P.tt(xT[:, co, :], xT[:, co, :], po[co][:], ALU.add)
        if "ffn" in hooks:
            hooks["ffn"]()
        rs_ = rmsnorm_to_hT(l, C_G2)
        for kt in range(8):
            P.stt(hT[:, kt, :], xT[:, kt, :], cl(C_G2 + kt), rs_, ALU.mult, ALU.mult)
        for j in range(8):
            i1 = pidx[("F1", j)]
            touch(i1)
            W1 = ring_view(i1, [128, 8, 512])
            for c4 in range(4):
                pf = ps()
                for kt in range(8):
                    P.mm(pf[:], W1[:, kt, c4 * 128:(c4 + 1) * 128], hT[:, kt, :], start=(kt == 0), stop=(kt == 7))
                r_ = rl[c4 % 2]
                P.act(r_, pf[:], AF.Relu)
                ui = j * 4 + c4
                P.tt(uT[:, ui * 512:(ui + 1) * 512], r_, r_, ALU.mult, eng="pool")
        if "ff2" in hooks:
            hooks["ff2"]()
        po = [ps() for _ in range(8)]
        for q in range(8):
            i2 = pidx[("F2", q)]
            touch(i2)
            W2 = ring_view(i2, [128, 4, 1024])
            for co in range(8):
                for kk in range(4):
                    ui = q * 4 + kk
                    P.mm(po[co][:], W2[:, kk, co * 128:(co + 1) * 128], uT[:, ui * 512:(ui + 1) * 512],
                         start=(q == 0 and kk == 0), stop=(q == 7 and kk == 3))
        for co in range(8):
            P.tt(xT[:, co, :], xT[:, co, :], po[co][:], ALU.add)

    stage3 = stage.rearrange("p (c d) -> p c d", d=T)
    ostage3 = scr[:, 4096:8192].rearrange("p (c d) -> p c d", d=T)
    outs = []
    plans = [[[plan_layer(l) for l in range(DEPTH)] for ti in range(NT)] for b in range(NSEQ)]
    tiles = [(b, ti) for b in range(NSEQ) for ti in range(NT)]

    def xsrc(t_ap, b, ti):
        return bass.AP(t_ap.tensor, b * D * S + ti * T, [[S, 128], [128 * S, 8], [1, T]])

    for n, (b, ti) in enumerate(tiles):
        if n == 0:
            P.dma(stage3, xsrc(x_d, b, ti), eng="sp", grp="xin")
        for kt in range(8):
            P.copy(xT[:, kt, :], stage3[:, kt, :], eng=("act", "dve", "pool")[kt % 3])
        for l in range(DEPTH):
            hooks = {}
            if b == 0 and ti == 0 and l + 1 < DEPTH:
                hooks = {ph: (lambda ll=l + 1, bb=bi: prepass(ll, bb)) for bi, ph in enumerate(("a", "b", "c", "o"))}
            if l == DEPTH - 1 and n + 1 < len(tiles):
                nb, nti = tiles[n + 1]
                hooks["ffn"] = (lambda nb=nb, nti=nti: P.dma(stage3, xsrc(x_d, nb, nti), eng="sp", grp="xin"))
            block(l, ti == 0, plans[b][ti][l], hooks)
        rs_ = rmsnorm_to_hT(0, 0)
        for kt in range(8):
            P.stt(ostage3[:, kt, :], xT[:, kt, :], fin[:, kt:kt + 1], rs_, ALU.mult, ALU.mult)
        dst = xsrc(out_d, b, ti)
        P.dma(dst, ostage3, eng="sp", grp="xout")
        outs.append(dst)
    P.add("sp", lambda e: None, reads=outs)
    P.emit()
    return nc, P


def prep_weights(norm1_g, w_in, conv_a_w, conv_a_b, ln_a_g, ln_a_b, ln_b_g, ln_b_b, w_spatial, b_spatial,
                 conv_c_w, conv_c_b, dt_bias, a_log, d_skip, norm_c_g, w_out, norm2_g, w_ff1, w_ff2, final_g):
    L = w_in.shape[0]
    f = np.float32
    perm = []
    for ct in range(4):
        perm += list(range(ct * 128, (ct + 1) * 128)) + list(range(512 + ct * 128, 512 + (ct + 1) * 128))
    perm += list(range(1024, DIN))
    w_in_p = np.ascontiguousarray(np.asarray(w_in, f)[:, :, perm])
    cols = np.zeros((L, 128, NCOLS), f)
    colv = lambda v, n: np.asarray(v, f).reshape(L, n, 128).transpose(0, 2, 1)
    cols[:, :, C_G1:C_G1 + 8] = colv(norm1_g, 8)
    cols[:, :, C_G2:C_G2 + 8] = colv(norm2_g, 8)
    cols[:, :, C_CAB:C_CAB + 4] = colv(conv_a_b, 4)
    cols[:, :, C_LAG:C_LAG + 4] = colv(ln_a_g, 4)
    cols[:, :, C_LAB:C_LAB + 4] = colv(ln_a_b, 4)
    cols[:, :, C_CCB:C_CCB + 4] = colv(np.asarray(conv_c_b, f)[:, 1024:1536], 4)
    cols[:, :, C_BST:C_BST + 8] = np.asarray(b_spatial, f).transpose(0, 2, 1)
    cols[:, :, C_NCG:C_NCG + 8] = colv(norm_c_g, 8)
    caw = np.asarray(conv_a_w, f).reshape(L, 31, 4, 128).transpose(0, 3, 2, 1)
    cols[:, :, C_CAW:C_CAW + 124] = caw.reshape(L, 128, 124)
    ccw = np.asarray(conv_c_w, f).reshape(L, 4, 12, 128).transpose(0, 3, 2, 1)
    cols[:, :, C_CCW:C_CCW + 48] = ccw.reshape(L, 128, 48)
    rows = np.concatenate([np.asarray(ln_b_g, f), np.asarray(ln_b_b, f), np.asarray(conv_c_b, f),
                           np.asarray(dt_bias, f), np.asarray(a_log, f), np.asarray(d_skip, f)], axis=1)
    assert rows.shape[1] == NROWS
    wst = np.ascontiguousarray(np.asarray(w_spatial, f).transpose(0, 3, 1, 2)).reshape(L, 128, 1024)
    fin = np.ascontiguousarray(np.asarray(final_g, f).reshape(8, 128).T)
    return dict(w_in=w_in_p, w_out=np.ascontiguousarray(np.asarray(w_out, f)),
                w_ff1=np.ascontiguousarray(np.asarray(w_ff1, f)), w_ff2=np.ascontiguousarray(np.asarray(w_ff2, f)),
                cols=cols, rows=np.ascontiguousarray(rows), wst=wst, fin=fin)


_CACHE = {}


def run(x, weights, n_cores, trace=False):
    x = np.asarray(x, np.float32)
    B, S, _ = x.shape
    NSEQ = B // n_cores
    DEPTH = weights["w_in"].shape[0]
    key = (NSEQ, S, DEPTH)
    if key not in _CACHE:
        _CACHE[key] = build_nc(NSEQ, S, DEPTH)[0]
    nc = _CACHE[key]
    in_maps = []
    for c in range(n_cores):
        m = dict(weights)
        m["x"] = np.ascontiguousarray(x[c * NSEQ:(c + 1) * NSEQ].transpose(0, 2, 1))
        in_maps.append(m)
    res = run_bass_kernel_spmd(nc, in_maps, core_ids=list(range(n_cores)))
    return np.ascontiguousarray(np.concatenate([r["out"] for r in res.results], axis=0).transpose(0, 2, 1))


def kernel(x, **params):
    weights = prep_weights(**params)
    return run(x, weights, 8)
```
